# Optimizing a Trainium2 kernel written in Bass

```python
import math
import jax, jax.numpy as jnp
from jax import lax
import numpy as np

D_MODEL = 1024
BATCH = 32
SEQ = 256
DEPTH = 4
DEC_BATCH = 4
DEC_SEQ = 2048
PAST_LEN = 256

GRID_W = 64
N_MIXERS = 2
N_FOURIER_LAYERS = (DEPTH + 1) // 2
N_ATTN_LAYERS = DEPTH // 2
N_HEADS = 8
HEAD_DIM = 64
V_DIM = 2 * HEAD_DIM
N_FOURIER_GROUPS = 4
FOURIER_GROUP = D_MODEL // N_FOURIER_GROUPS
D_FF = 2816
CONV_WIDTH = 3
ROPE_THETA = 10000.0
AXIS_DIM = HEAD_DIM // 2
N_FREQ = AXIS_DIM // 2
Q_BLOCK = 128
DN_ALPHA = (2 * DEPTH) ** 0.25
DN_BETA = (8 * DEPTH) ** -0.25
LN_EPS = 1e-6
SUBLN_EPS = 1e-5

kernel_name = "hybrid_fourier_diffattn_dit_step"


def _ln_f32(x):
    x32 = x.astype(jnp.float32)
    mu = jnp.mean(x32, axis=-1, keepdims=True)
    var = jnp.mean(jnp.square(x32 - mu), axis=-1, keepdims=True)
    return (x32 - mu) * lax.rsqrt(var + LN_EPS)


def modulate(x, shift, scale):
    return (_ln_f32(x) * (1.0 + scale.astype(jnp.float32)) + shift.astype(jnp.float32)).astype(x.dtype)


def post_norm(x, update, g, b):
    z = DN_ALPHA * x.astype(jnp.float32) + update.astype(jnp.float32)
    y = _ln_f32(z) * g.astype(jnp.float32) + b.astype(jnp.float32)
    return y.astype(x.dtype)


def ada_params(cvec, w, b):
    a = jax.nn.silu(cvec) @ w + b
    return jnp.split(a[:, None, :], 6, axis=-1)


def fourier_mix(h, w_f):
    bn, n, d = h.shape
    hg = h.astype(jnp.float32).reshape(bn, n, N_FOURIER_GROUPS, FOURIER_GROUP)
    f = jnp.fft.fft2(hg, axes=(1, 3), norm="ortho").real
    return f.reshape(bn, n, d).astype(h.dtype) @ w_f


def conv_ffn(h, w_up, conv_w, conv_b, w_down):
    n = h.shape[1]
    u = h @ w_up
    up = jnp.pad(u, ((0, 0), (1, 1), (0, 0)))
    u = up[:, 0:n] * conv_w[0] + up[:, 1:n + 1] * conv_w[1] + up[:, 2:n + 2] * conv_w[2] + conv_b
    a, g = jnp.split(u, 2, axis=-1)
    return (jax.nn.silu(a) * g) @ w_down


def axial_rope_tables(n, dtype):
    rows = n // GRID_W
    row = jnp.repeat(jnp.arange(rows), GRID_W).astype(jnp.float32)
    col = jnp.tile(jnp.arange(GRID_W), rows).astype(jnp.float32)
    inv = 1.0 / (ROPE_THETA ** (jnp.arange(N_FREQ, dtype=jnp.float32) / N_FREQ))
    ar = row[:, None] * inv
    ac = col[:, None] * inv
    ang = jnp.concatenate([ar, ar, ac, ac], axis=-1)
    return jnp.cos(ang).astype(dtype), jnp.sin(ang).astype(dtype)


def rotate_half_axial(x):
    xs = x.reshape(x.shape[:-1] + (2, 2, N_FREQ))
    xs = jnp.concatenate([-xs[..., 1:, :], xs[..., :1, :]], axis=-2)
    return xs.reshape(x.shape)


def apply_rope(x, cos, sin):
    c = cos[None, :, None, None, :]
    s = sin[None, :, None, None, :]
    return x * c + rotate_half_axial(x) * s


def qkv_proj(h, w):
    bn, n, _ = h.shape
    q, k, v = jnp.split(h @ w, 3, axis=-1)
    return (q.reshape(bn, n, N_HEADS, 2, HEAD_DIM),
            k.reshape(bn, n, N_HEADS, 2, HEAD_DIM),
            v.reshape(bn, n, N_HEADS, V_DIM))


def diff_lambda(lq1, lk1, lq2, lk2, lam_init):
    f = jnp.float32
    return (jnp.exp(jnp.sum(lq1.astype(f) * lk1.astype(f)))
            - jnp.exp(jnp.sum(lq2.astype(f) * lk2.astype(f))) + lam_init)


def diff_attention(q, k, v, lam):
    bn, nq = q.shape[0], q.shape[1]
    nblk = nq // Q_BLOCK
    qb = jnp.moveaxis(q.reshape(bn, nblk, Q_BLOCK, N_HEADS, 2, HEAD_DIM), 1, 0)
    scale = HEAD_DIM ** -0.5

    def one_block(qblk):
        s = jnp.einsum('bqhtd,bkhtd->bhtqk', qblk, k).astype(jnp.float32) * scale
        p = jax.nn.softmax(s, axis=-1)
        w = p[:, :, 0] - lam * p[:, :, 1]
        return jnp.einsum('bhqk,bkhe->bqhe', w.astype(v.dtype), v)

    out = lax.map(one_block, qb)
    return jnp.moveaxis(out, 0, 1).reshape(bn, nq, N_HEADS, V_DIM)


def diff_attn_output(o, g, lam_init, w_o):
    bn, n = o.shape[0], o.shape[1]
    o32 = o.astype(jnp.float32)
    o32 = o32 * lax.rsqrt(jnp.mean(jnp.square(o32), axis=-1, keepdims=True) + SUBLN_EPS)
    o32 = o32 * g.astype(jnp.float32) * (1.0 - lam_init)
    return o32.reshape(bn, n, D_MODEL).astype(o.dtype) @ w_o


def setup_inputs(seed: int = 0) -> dict:
    key = jax.random.key(seed)
    ks = jax.random.split(key, 24)
    nrm = jax.random.normal
    f = jnp.float32
    d = D_MODEL
    v_col_scale = jnp.concatenate([jnp.ones((2 * d,), f), jnp.full((d,), DN_BETA, f)])
    return {
        "x_prompt": nrm(ks[0], (BATCH, SEQ, d), f),
        "x_sample": nrm(ks[1], (DEC_BATCH, DEC_SEQ, d), f),
        "cache_k": nrm(ks[2], (DEC_BATCH, N_ATTN_LAYERS, PAST_LEN, N_HEADS, 2, HEAD_DIM), f),
        "cache_v": nrm(ks[3], (DEC_BATCH, N_ATTN_LAYERS, PAST_LEN, N_HEADS, V_DIM), f) * DN_BETA,
        "c": nrm(ks[4], (DEC_BATCH, d), f),
        "c_ctx": nrm(ks[5], (d,), f),
        "w_ada": nrm(ks[6], (DEPTH, d, 6 * d), f) * d ** -0.5,
        "b_ada": nrm(ks[7], (DEPTH, 6 * d), f) * 0.02,
        "w_fourier": nrm(ks[8], (N_FOURIER_LAYERS, d, d), f) * (d ** -0.5 * DN_BETA),
        "w_qkv": nrm(ks[9], (N_ATTN_LAYERS, d, 3 * d), f) * d ** -0.5 * v_col_scale,
        "lambda_q1": nrm(ks[10], (N_ATTN_LAYERS, HEAD_DIM), f) * 0.1,
        "lambda_k1": nrm(ks[11], (N_ATTN_LAYERS, HEAD_DIM), f) * 0.1,
        "lambda_q2": nrm(ks[12], (N_ATTN_LAYERS, HEAD_DIM), f) * 0.1,
        "lambda_k2": nrm(ks[13], (N_ATTN_LAYERS, HEAD_DIM), f) * 0.1,
        "subln_g": 1.0 + 0.02 * nrm(ks[14], (N_ATTN_LAYERS, V_DIM), f),
        "w_o": nrm(ks[15], (N_ATTN_LAYERS, d, d), f) * (d ** -0.5 * DN_BETA),
        "w_up": nrm(ks[16], (DEPTH, d, 2 * D_FF), f) * (d ** -0.5 * DN_BETA),
        "conv_w": nrm(ks[17], (DEPTH, CONV_WIDTH, 2 * D_FF), f) * CONV_WIDTH ** -0.5,
        "conv_b": nrm(ks[18], (DEPTH, 2 * D_FF), f) * 0.02,
        "w_down": nrm(ks[19], (DEPTH, D_FF, d), f) * (D_FF ** -0.5 * DN_BETA),
        "ln1_g": 1.0 + 0.02 * nrm(ks[20], (DEPTH, d), f),
        "ln1_b": 0.02 * nrm(ks[21], (DEPTH, d), f),
        "ln2_g": 1.0 + 0.02 * nrm(ks[22], (DEPTH, d), f),
        "ln2_b": 0.02 * nrm(ks[23], (DEPTH, d), f),
    }


def reference(x_prompt, x_sample, cache_k, cache_v, c, c_ctx, w_ada, b_ada, w_fourier, w_qkv,
              lambda_q1, lambda_k1, lambda_q2, lambda_k2, subln_g, w_o, w_up, conv_w, conv_b,
              w_down, ln1_g, ln1_b, ln2_g, ln2_b):
    n_lat = x_sample.shape[1]
    cos, sin = axial_rope_tables(n_lat, x_sample.dtype)
    c_ctx_row = c_ctx[None, :]
    xc, xl = x_prompt, x_sample
    new_k, new_v = [], []
    for i in range(DEPTH):
        j = i // N_MIXERS
        sh1c, sc1c, g1c, sh2c, sc2c, g2c = ada_params(c_ctx_row, w_ada[i], b_ada[i])
        sh1l, sc1l, g1l, sh2l, sc2l, g2l = ada_params(c, w_ada[i], b_ada[i])
        hc = modulate(xc, sh1c, sc1c)
        hl = modulate(xl, sh1l, sc1l)
        if i % N_MIXERS == 0:
            mc = fourier_mix(hc, w_fourier[j])
            ml = fourier_mix(hl, w_fourier[j])
        else:
            lam_init = 0.8 - 0.6 * math.exp(-0.3 * i)
            lam = diff_lambda(lambda_q1[j], lambda_k1[j], lambda_q2[j], lambda_k2[j], lam_init)
            qc, kc, vc = qkv_proj(hc, w_qkv[j])
            oc = diff_attention(qc, kc, vc, lam)
            mc = diff_attn_output(oc, subln_g[j], lam_init, w_o[j])
            new_k.append(kc)
            new_v.append(vc)
            ql, kl, vl = qkv_proj(hl, w_qkv[j])
            ql = apply_rope(ql, cos, sin)
            kl = apply_rope(kl, cos, sin)
            k_all = jnp.concatenate([kl, cache_k[:, j]], axis=1)
            v_all = jnp.concatenate([vl, cache_v[:, j]], axis=1)
            ol = diff_attention(ql, k_all, v_all, lam)
            ml = diff_attn_output(ol, subln_g[j], lam_init, w_o[j])
        xc = post_norm(xc, g1c * mc, ln1_g[i], ln1_b[i])
        xl = post_norm(xl, g1l * ml, ln1_g[i], ln1_b[i])
        fc = conv_ffn(modulate(xc, sh2c, sc2c), w_up[i], conv_w[i], conv_b[i], w_down[i])
        fl = conv_ffn(modulate(xl, sh2l, sc2l), w_up[i], conv_w[i], conv_b[i], w_down[i])
        xc = post_norm(xc, g2c * fc, ln2_g[i], ln2_b[i])
        xl = post_norm(xl, g2l * fl, ln2_g[i], ln2_b[i])
    new_cache_k = jnp.stack(new_k, axis=1)
    new_cache_v = jnp.stack(new_v, axis=1)
    return (xc, xl, new_cache_k, new_cache_v)
```

```python
import math
from contextlib import ExitStack
import numpy as np
import ml_dtypes
import concourse.bass as bass
import concourse.mybir as mybir
from concourse.bass_utils import run_bass_kernel_spmd

F32 = mybir.dt.float32
BF16 = mybir.dt.bfloat16
AF = mybir.ActivationFunctionType
ALU = mybir.AluOpType
AX = mybir.AxisListType

D = 1024
NT = 2048
DEPTH = 4
DFF = 2816
NCH = 22
ALPHA = (2 * DEPTH) ** 0.25
LN_EPS = 1e-6
SUB_EPS = 1e-5
NEG = -30000.0


class Sem:
    def __init__(self, h, dma=False):
        self.h = h
        self.total = 0
        self.dma = dma


class Dep:
    __slots__ = ("w", "r", "x")

    def __init__(self, x=False):
        self.w = None
        self.r = []
        self.x = x


class Eng:
    def __init__(self, name, h, sem):
        self.name = name
        self.h = h
        self.sem = sem
        self.known = {}
        self.pending = False


class MK:
    def __init__(self, nc, stack):
        self.nc = nc
        self.stack = stack
        self.engs = {}
        for name, h in (("pe", nc.tensor), ("act", nc.scalar), ("dve", nc.vector),
                        ("pool", nc.gpsimd), ("sp", nc.sync)):
            s = Sem(stack.enter_context(nc.semaphore("s_" + name)))
            self.engs[name] = Eng(name, h, s)
        self.dma_sems = []
        self.depsem = {}
        self.n_inst = 0
        self.n_wait = 0

    def new_dma_sem(self, name):
        s = Sem(self.stack.enter_context(self.nc.semaphore(name)), dma=True)
        self.dma_sems.append(s)
        return s

    def _wait(self, eng, evs):
        need = {}
        for ev in evs:
            if ev is None:
                continue
            s, v = ev
            if s.dma:
                v = s.total
            if s is eng.sem and eng.name == "pe":
                continue
            if eng.known.get(s, 0) >= v:
                continue
            if need.get(s, 0) < v:
                need[s] = v
        for s, v in need.items():
            eng.h.wait_ge(s.h, v)
            eng.known[s] = v
            self.n_wait += 1

    def dsem(self, dep):
        k = id(dep)
        if k not in self.depsem:
            self.depsem[k] = (self.new_dma_sem(f"d_auto{len(self.depsem)}"), dep)
        return self.depsem[k][0]

    def emit(self, eng, fn, reads=(), writes=(), sig=True, dma_sem=None):
        e = self.engs[eng]
        reads = _flat(reads)
        writes = _flat(writes)
        if dma_sem == "auto":
            dma_sem = self.dsem(writes[0] if len(writes) else reads[0])
        if any(d.x for d in reads):
            writes = list(writes) + [d for d in reads if d.x]
            reads = [d for d in reads if not d.x]
        evs = []
        for d in reads:
            evs.append(d.w)
        for d in writes:
            if dma_sem is not None and d.w is not None and d.w[0] is dma_sem and not d.r:
                pass
            else:
                evs.append(d.w)
            evs.extend(d.r)
        self._wait(e, evs)
        inst = fn(e.h)
        self.n_inst += 1
        if dma_sem is not None:
            dma_sem.total += 16
            inst.then_inc(dma_sem.h, 16)
            ev = (dma_sem, dma_sem.total)
        elif sig:
            e.sem.total += 1
            inst.then_inc(e.sem.h, 1)
            ev = (e.sem, e.sem.total)
            e.pending = False
        else:
            ev = (e.sem, e.sem.total + 1)
            e.pending = True
        for d in reads:
            d.r.append(ev)
            if len(d.r) > 48:
                best = {}
                for s, v in d.r:
                    if best.get(s, 0) < v:
                        best[s] = v
                d.r = list(best.items())
        for d in writes:
            d.w = ev
            d.r = []
        return inst

    def barrier(self):
        sems = [e.sem for e in self.engs.values()] + self.dma_sems
        for e in self.engs.values():
            assert not e.pending, e.name
            for s in sems:
                if s is e.sem or s.total == 0:
                    continue
                if e.known.get(s, 0) >= s.total:
                    continue
                e.h.wait_ge(s.h, s.total)
                e.known[s] = s.total

    def finish(self):
        self.barrier()


def _flat(ds):
    out = []
    for d in ds:
        if isinstance(d, (tuple, list)):
            out.extend(_flat(d))
        else:
            out.append(d)
    return out


class Ring:
    def __init__(self, items):
        self.items = list(items)
        self.i = 0

    def next(self):
        it = self.items[self.i % len(self.items)]
        self.i += 1
        return it


class StopBuild(Exception):
    pass


def build_program(stop=None, att_stop=None):
    nc = bass.Bass("TRN2", target_bir_lowering=False)

    def din(name, shape, dt=F32):
        return nc.dram_tensor(name, list(shape), dt, kind="ExternalInput").ap()

    def dout(name, shape):
        return nc.dram_tensor(name, list(shape), F32, kind="ExternalOutput").ap()

    x_d = din("x", [NT, D])
    c_d = din("cvec", [8, 128])
    ck_d = din("ck", [2, 256, D])
    cv_d = din("cv", [2, 256, D])
    flag_d = din("flag", [128, 1])
    qaux_d = din("qaux", [64, NT], BF16)
    kaux_d = din("kaux", [64, NT + 256], BF16)
    cos_d = din("cosT", [128, NT])
    sin_d = din("sinT", [128, NT])
    dft_d = din("dft", [16, 128, 16, 2, 128], BF16)
    ccs_d = din("ccs", [128, 2, 512], BF16)
    ident_d = din("ident", [128, 128], BF16)
    identf_d = din("identf", [128, 128])
    rotm_d = din("rotm", [128, 128], BF16)
    w_ada = din("w_ada", [DEPTH, D, 6 * D])
    b_ada = din("b_ada", [DEPTH, 48, 128])
    w_fourier = din("w_fourier", [2, D, D])
    w_qkv = din("w_qkv", [2, D, 3 * D])
    lamp = din("lamp", [2, 4, 64])
    subln_g = din("subln_g", [2, 128])
    w_o = din("w_o", [2, D, D])
    w_up = din("w_up", [DEPTH, D, 2 * DFF])
    conv_w = din("conv_w", [DEPTH, 3, 44, 128])
    conv_b = din("conv_b", [DEPTH, 44, 128])
    w_down = din("w_down", [DEPTH, DFF, D])
    ln_g = [din("ln1_g", [DEPTH, D]), din("ln2_g", [DEPTH, D])]
    ln_b = [din("ln1_b", [DEPTH, D]), din("ln2_b", [DEPTH, D])]
    y_d = dout("y", [NT, D])
    nk_d = dout("nk", [NT, 2, D])
    nv_d = dout("nv", [NT, 2, D])

    with ExitStack() as st:
        mk = MK(nc, st)
        E = mk.emit

        tcount = [0]

        def T(name, shape, dt, stack=st):
            tcount[0] += 1
            return stack.enter_context(nc.sbuf_tensor(f"{name}_{tcount[0]}", list(shape), dt))

        psum = st.enter_context(nc.psum_tensor("psum_all", [128, 4096], F32))
        banks = [psum[:, i * 512:(i + 1) * 512] for i in range(8)]
        bdep = [Dep(x=True) for _ in range(8)]

        def bbf(i):
            return banks[i].bitcast(BF16)

        xt = T("xt", [128, 16, D], F32)
        xd = [Dep() for _ in range(16)]
        hT = T("hT", [128, 8, NT + 2], BF16)
        hd = [tuple(Dep() for _ in range(8)) for _ in range(4)]
        nt = T("nt", [128, 4, D], BF16)
        ntd = [Dep() for _ in range(4)]
        gbt = T("gbt", [128, D], F32)
        gbd = Dep()
        lng = T("lng", [128, D], F32)
        lnb = T("lnb", [128, D], F32)
        lnd = Dep()
        ident = T("ident_sb", [128, 128], BF16)
        identf = T("identf_sb", [128, 128], F32)
        onesf = T("onesf", [128, 128], F32)
        rotm = T("rotm_sb", [128, 128], BF16)
        cst = Dep()
        adaT = T("adaT", [128, 48], F32)
        adad = Dep()
        sct = T("sct", [128, 8], BF16)
        misc = T("misc", [128, 16], F32)
        miscd = Dep()
        NSV = 6
        svt = T("svt", [128, NSV, 16], F32)
        svring = Ring([(svt[:, s, :], Dep()) for s in range(NSV)])
        dgt = T("dgt", [128, 2, 128], F32)
        dgring = Ring([(dgt[:, s, :], Dep()) for s in range(2)])

        s_in = mk.new_dma_sem("d_in")
        s_out = mk.new_dma_sem("d_out")
        s_ln = mk.new_dma_sem("d_ln")
        s_misc = mk.new_dma_sem("d_misc")

        xin = x_d.rearrange("(t p) d -> p t d", p=128)
        for tt in range(16):
            E("sp", lambda e: e.dma_start(out=xt[:, tt, :], in_=xin[:, tt, :]), writes=[xd[tt]], dma_sem=s_in)
        E("sp", lambda e: e.dma_start(out=ident[:], in_=ident_d), writes=[cst], dma_sem="auto")
        E("sp", lambda e: e.dma_start(out=identf[:], in_=identf_d), writes=[cst], dma_sem="auto")
        E("sp", lambda e: e.dma_start(out=rotm[:], in_=rotm_d), writes=[cst], dma_sem="auto")
        E("sp", lambda e: e.dma_start(out=misc[:, 0:1], in_=flag_d), writes=[miscd], dma_sem="auto")
        E("pool", lambda e: e.memset(onesf[:], 1.0), writes=[cst])
        E("pool", lambda e: e.memset(hT[:, :, 0:1], 0.0), writes=hd)
        E("pool", lambda e: e.memset(hT[:, :, NT + 1:NT + 2], 0.0), writes=hd)
        E("dve", lambda e: e.tensor_scalar(out=misc[:, 1:2], in0=misc[:, 0:1], scalar1=-1.0, scalar2=1.0, op0=ALU.mult, op1=ALU.add),
          reads=[miscd], writes=[miscd])
        E("dve", lambda e: e.tensor_scalar(out=misc[:, 2:3], in0=misc[:, 0:1], scalar1=-1.0, scalar2=None, op0=ALU.mult),
          reads=[miscd], writes=[miscd])
        E("dve", lambda e: e.memset(misc[:, 3:4], LN_EPS), writes=[miscd])
        E("dve", lambda e: e.memset(misc[:, 4:5], SUB_EPS), writes=[miscd])
        with ExitStack() as ps:
            c8 = T("c8", [8, 128], F32, ps)
            c8d = Dep()
            E("sp", lambda e: e.dma_start(out=c8[:], in_=c_d), writes=[c8d], dma_sem="auto")
            E("pe", lambda e: e.transpose(out=banks[0][:, 0:8], in_=c8[:], identity=identf[0:8, 0:8]),
              reads=[c8d, cst], writes=[bdep[0]])
            E("act", lambda e: e.activation(out=sct[:], in_=banks[0][:, 0:8], func=AF.Silu), reads=[bdep[0]], writes=[adad])
            mk.barrier()

        def ln_stats(src, deps):
            sv, dep = svring.next()
            for c in range(2):
                E("dve", lambda e: e.bn_stats(out=sv[:, c * 6:(c + 1) * 6], in_=src[:, c * 512:(c + 1) * 512]),
                  reads=deps, writes=[dep])
            E("dve", lambda e: e.bn_aggr(out=sv[:, 12:14], in_=sv[:, 0:12]), reads=[dep], writes=[dep])
            E("dve", lambda e: e.tensor_scalar(out=sv[:, 14:15], in0=sv[:, 13:14], scalar1=LN_EPS, scalar2=None, op0=ALU.add),
              reads=[dep], writes=[dep])
            E("act", lambda e: e.activation(out=sv[:, 14:15], in_=sv[:, 14:15], func=AF.Sqrt), reads=[dep], writes=[dep])
            E("dve", lambda e: e.reciprocal(out=sv[:, 14:15], in_=sv[:, 14:15]), reads=[dep], writes=[dep])
            E("dve", lambda e: e.scalar_tensor_tensor(out=sv[:, 15:16], in0=sv[:, 12:13], scalar=-1.0, in1=sv[:, 14:15],
                                                      op0=ALU.mult, op1=ALU.mult), reads=[dep], writes=[dep])
            return sv[:, 14:15], sv[:, 15:16], dep

        def load_ln(which, i):
            E("sp", lambda e: e.dma_start(out=lng[:], in_=ln_g[which][i].partition_broadcast(128)), writes=[lnd], dma_sem=s_ln)
            E("sp", lambda e: e.dma_start(out=lnb[:], in_=ln_b[which][i].partition_broadcast(128)), writes=[lnd], dma_sem=s_ln)

        def post_norm(tt, b0, b1):
            for h, b in ((0, b0), (1, b1)):
                E("dve", lambda e: e.scalar_tensor_tensor(out=xt[:, tt, h * 512:(h + 1) * 512], in0=xt[:, tt, h * 512:(h + 1) * 512],
                                                          scalar=ALPHA, in1=banks[b][:, :], op0=ALU.mult, op1=ALU.add),
                  reads=[bdep[b]], writes=[xd[tt]])
            rstd, nb, dep = ln_stats(xt[:, tt, :], [xd[tt]])
            E("act", lambda e: e.activation(out=xt[:, tt, :], in_=xt[:, tt, :], func=AF.Identity, bias=nb, scale=rstd),
              reads=[dep], writes=[xd[tt]])
            E("pool", lambda e: e.tensor_tensor(out=xt[:, tt, :], in0=xt[:, tt, :], in1=lng[:], op=ALU.mult),
              reads=[lnd], writes=[xd[tt]])
            E("pool", lambda e: e.tensor_tensor(out=xt[:, tt, :], in0=xt[:, tt, :], in1=lnb[:], op=ALU.add),
              reads=[lnd], writes=[xd[tt]])

        tpring = Ring([7, 4])
        evring = Ring(["act", "dve"])

        def ln_mod(tg, sc0, sh0):
            for u in range(4):
                tt = tg * 4 + u
                rstd, nb, dep = ln_stats(xt[:, tt, :], [xd[tt]])
                E("act", lambda e: e.activation(out=nt[:, u, :], in_=xt[:, tt, :], func=AF.Identity, bias=nb, scale=rstd),
                  reads=[xd[tt], dep], writes=[ntd[u]])
            for j in range(8):
                b = tpring.next()
                pv = bbf(b)
                for u in range(4):
                    E("pe", lambda e: e.transpose(out=pv[:, u * 128:(u + 1) * 128], in_=nt[:, u, j * 128:(j + 1) * 128], identity=ident[:]),
                      reads=[ntd[u], cst], writes=[bdep[b]], sig=(u == 3))
                dst = hT[:, j, 1 + tg * 512:1 + (tg + 1) * 512]
                if evring.next() == "act":
                    E("act", lambda e: e.activation(out=dst, in_=pv[:, 0:512], func=AF.Identity,
                                                    bias=adaT[:, sh0 + j:sh0 + j + 1], scale=adaT[:, sc0 + j:sc0 + j + 1]),
                      reads=[bdep[b], adad], writes=[hd[tg][j]])
                else:
                    E("dve", lambda e: e.tensor_scalar(out=dst, in0=pv[:, 0:512], scalar1=adaT[:, sc0 + j:sc0 + j + 1],
                                                       scalar2=adaT[:, sh0 + j:sh0 + j + 1], op0=ALU.mult, op1=ALU.add),
                      reads=[bdep[b], adad], writes=[hd[tg][j]])

        def make_gb(g0, b0, b1):
            for j in range(8):
                dg, dgd = dgring.next()
                E("dve", lambda e: e.tensor_scalar(out=dg, in0=identf[:], scalar1=adaT[:, g0 + j:g0 + j + 1], scalar2=None, op0=ALU.mult),
                  reads=[cst, adad], writes=[dgd])
                b = b0 if j < 4 else b1
                E("pe", lambda e: e.matmul(banks[b][:, (j % 4) * 128:(j % 4 + 1) * 128], lhsT=onesf[:], rhs=dg, start=True, stop=True),
                  reads=[cst, dgd], writes=[bdep[b]])
            E("dve", lambda e: e.tensor_copy(out=gbt[:, 0:512], in_=banks[b0][:, :]), reads=[bdep[b0]], writes=[gbd])
            E("dve", lambda e: e.tensor_copy(out=gbt[:, 512:1024], in_=banks[b1][:, :]), reads=[bdep[b1]], writes=[gbd])

        def ada_phase(i):
            with ExitStack() as ps:
                slots = [(T(f"wada{s}", [128, 8, 512], BF16, ps), Dep(), mk.new_dma_sem(f"d_ada{i}_{s}")) for s in range(2)]
                bA = T("bA", [48, 128], F32, ps)
                bAd = Dep()
                bT = T("bT", [128, 48], F32, ps)
                E("sp", lambda e: e.dma_start(out=bA[:], in_=b_ada[i]), writes=[bAd], dma_sem="auto")
                first = True
                for blk in range(12):
                    sl, sd, ss = slots[blk % 2]
                    E("pool", lambda e: e.dma_start(out=sl[:], in_=w_ada[i][:, blk * 512:(blk + 1) * 512].rearrange("(k p) n -> p k n", p=128)),
                      writes=[sd], dma_sem=ss)
                    for cc in range(4):
                        j = blk * 4 + cc
                        for k in range(8):
                            E("pe", lambda e: e.matmul(banks[0][:, j:j + 1], lhsT=sl[:, k, cc * 128:(cc + 1) * 128], rhs=sct[:, k:k + 1],
                                                       start=first, stop=(k == 7), skip_group_check=True),
                              reads=[sd, adad], writes=[bdep[0]], sig=(k == 7 and cc == 3))
                            first = False
                E("pe", lambda e: e.transpose(out=banks[1][:, 0:48], in_=bA[:], identity=identf[0:48, 0:48]),
                  reads=[bAd, cst], writes=[bdep[1]])
                E("dve", lambda e: e.tensor_copy(out=bT[:], in_=banks[1][:, 0:48]), reads=[bdep[1]], writes=[bAd])
                E("dve", lambda e: e.tensor_tensor(out=adaT[:], in0=banks[0][:, 0:48], in1=bT[:], op=ALU.add),
                  reads=[bdep[0], bAd], writes=[adad])
                for c0 in (8, 32):
                    E("dve", lambda e: e.tensor_scalar(out=adaT[:, c0:c0 + 8], in0=adaT[:, c0:c0 + 8], scalar1=1.0, scalar2=None, op0=ALU.add),
                      reads=[adad], writes=[adad])
                mk.barrier()

        pmring = Ring([(5, 6), (2, 3)])
        cpring = Ring(["act", "dve"])

        def evac_copy(dst, src, reads, writes):
            if cpring.next() == "act":
                E("act", lambda e: e.activation(out=dst, in_=src, func=AF.Copy), reads=reads, writes=writes)
            else:
                E("dve", lambda e: e.tensor_copy(out=dst, in_=src), reads=reads, writes=writes)

        def fourier_phase(i, jf):
            with ExitStack() as ps:
                M = T("fM", [128, 8, D], BF16, ps)
                Md = Dep()
                Ap = T("fAp", [128, 16, D], BF16, ps)
                Apd = [Dep() for _ in range(16)]
                Bp = T("fBp", [128, 16, D], BF16, ps)
                Bpd = [Dep() for _ in range(16)]
                ccs = T("ccs_sb", [128, 2, 512], BF16, ps)
                ccd = Dep()
                wf = Bp
                s_wf = mk.new_dma_sem(f"d_wf{i}")
                s_dft = [mk.new_dma_sem(f"d_dft{i}_{s}") for s in range(4)]
                E("sp", lambda e: e.dma_start(out=ccs[:], in_=ccs_d), writes=[ccd], dma_sem="auto")
                E("pool", lambda e: e.dma_start(out=wf[:, 0:8, :], in_=w_fourier[jf].rearrange("(k p) n -> p k n", p=128)),
                  writes=Bpd[0:8], dma_sem=s_wf)
                make_gb(16, 0, 1)
                bring = Ring([0, 1, 2, 3])

                def make_M(coff):
                    for g in range(4):
                        for cch in range(2):
                            for dh in range(2):
                                b = bring.next()
                                for kc in range(2):
                                    E("pe", lambda e: e.matmul(banks[b][:, :], lhsT=ccs[:, kc, coff + cch * 128:coff + (cch + 1) * 128],
                                                               rhs=wf[:, g * 2 + kc, dh * 512:(dh + 1) * 512], start=(kc == 0), stop=(kc == 1)),
                                      reads=[ccd] + Bpd[0:8], writes=[bdep[b]], sig=(kc == 1))
                                E("dve", lambda e: e.tensor_tensor(out=M[:, g * 2 + cch, dh * 512:(dh + 1) * 512], in0=banks[b][:, :],
                                                                   in1=gbt[:, dh * 512:(dh + 1) * 512], op=ALU.mult),
                                  reads=[bdep[b], gbd], writes=[Md])

                def project(dstT, dstd):
                    for tt in range(16):
                        for dh in range(2):
                            b = bring.next()
                            for k in range(8):
                                E("pe", lambda e: e.matmul(banks[b][:, :], lhsT=hT[:, k, 1 + tt * 128:1 + (tt + 1) * 128],
                                                           rhs=M[:, k, dh * 512:(dh + 1) * 512], start=(k == 0), stop=(k == 7)),
                                  reads=[hd[tt // 4], Md], writes=[bdep[b]], sig=(k == 7))
                            evac_copy(dstT[:, tt, dh * 512:(dh + 1) * 512], banks[b][:, :], [bdep[b]], [dstd[tt]])

                make_M(0)
                project(Ap, Apd)
                make_M(256)
                project(Bp, Bpd)
                hflat = hT[:].rearrange("p a b -> p (a b)")
                slots = [hflat[:, s * 4096:(s + 1) * 4096].rearrange("p (k s n) -> p k s n", k=16, s=2) for s in range(4)]
                sld = [Dep() for _ in range(4)]
                load_ln(0, i)
                for n in range(16):
                    sl, sd = slots[n % 4], sld[n % 4]
                    E("sp", lambda e: e.dma_start(out=sl, in_=dft_d[n]), writes=[sd] + (hd if n < 4 else []), dma_sem=s_dft[n % 4])
                    b0, b1 = pmring.next()
                    for dh, b in ((0, b0), (1, b1)):
                        for k in range(16):
                            for s in range(2):
                                src, srcd = (Ap, Apd) if s == 0 else (Bp, Bpd)
                                E("pe", lambda e: e.matmul(banks[b][:, :], lhsT=sl[:, k, s, :], rhs=src[:, k, dh * 512:(dh + 1) * 512],
                                                           start=(k == 0 and s == 0), stop=(k == 15 and s == 1)),
                                  reads=[sd, srcd[k]], writes=[bdep[b]], sig=(k == 15 and s == 1))
                    post_norm(n, b0, b1)
                E("pool", lambda e: e.memset(hT[:, :, 0:1], 0.0), writes=sld + hd)
                E("pool", lambda e: e.memset(hT[:, :, NT + 1:NT + 2], 0.0), writes=sld + hd)
                mk.barrier()

        def ffn_phase(i):
            with ExitStack() as ps:
                wd = T("wd", [128, NCH, D], BF16, ps)
                wdd = Dep()
                actT = T("actT", [128, NCH, 512], BF16, ps)
                actd = [Dep() for _ in range(NCH)]
                wus = [(T(f"wu{s}", [128, 8, 2, 128], BF16, ps), Dep(), mk.new_dma_sem(f"d_wu{i}_{s}")) for s in range(3)]
                accs = [(T(f"acca{s}", [128, 512], F32, ps), T(f"accg{s}", [128, 512], F32, ps), Dep(), Dep()) for s in range(2)]
                cwr = gbt[0:44, 0:512].rearrange("p (a b) -> p a b", a=4)
                cwrd = gbd
                cwT = T("cwT", [128, 4, 44], F32, ps)
                cwx = T("cwx", [128, 4, 44], F32, ps)
                cwd = Dep()
                s_wd = mk.new_dma_sem(f"d_wd{i}")
                E("pool", lambda e: e.dma_start(out=wd[:], in_=w_down[i].rearrange("(k p) n -> p k n", p=128)), writes=[wdd], dma_sem=s_wd)
                for s in range(3):
                    E("sp", lambda e: e.dma_start(out=cwr[:, s, :], in_=conv_w[i, s]), writes=[cwrd], dma_sem="auto")
                E("sp", lambda e: e.dma_start(out=cwr[:, 3, :], in_=conv_b[i]), writes=[cwrd], dma_sem="auto")
                for s in range(4):
                    E("pe", lambda e: e.transpose(out=banks[4][:, s * 44:(s + 1) * 44], in_=cwr[:, s, :], identity=identf[0:44, 0:44]),
                      reads=[cwrd, cst], writes=[bdep[4]])
                E("dve", lambda e: e.tensor_copy(out=cwT[:].rearrange("p a b -> p (a b)"), in_=banks[4][:, 0:176]), reads=[bdep[4]], writes=[cwd])
                for (o, tap, mcol) in ((0, 0, 1), (1, 2, 1), (2, 0, 2), (3, 2, 2)):
                    E("dve", lambda e: e.tensor_scalar(out=cwx[:, o, :], in0=cwT[:, tap, :], scalar1=misc[:, mcol:mcol + 1], scalar2=None, op0=ALU.mult),
                      reads=[cwd, miscd], writes=[cwd])
                make_gb(40, 5, 6)
                for k in range(NCH):
                    E("pool", lambda e: e.tensor_tensor(out=wd[:, k, :], in0=wd[:, k, :], in1=gbt[:], op=ALU.mult),
                      reads=[gbd], writes=[wdd])
                load_ln(1, i)
                wup = w_up[i].rearrange("(k p) (s n) -> p k s n", p=128, s=2)

                def load_wu(pidx):
                    if pidx >= 4 * NCH:
                        return
                    wu_, wud_, sem_ = wus[pidx % 3]
                    mp_ = pidx % NCH
                    for s_ in range(2):
                        E("pool", lambda e: e.dma_start(out=wu_[:, :, s_, :], in_=wup[:, :, s_, mp_ * 128:(mp_ + 1) * 128]), writes=[wud_], dma_sem=sem_)

                load_wu(0)
                load_wu(1)
                pair = 0
                for tq in range(4):
                    c0 = tq * 512
                    for mp in range(NCH):
                        wu, wud, wus_sem = wus[pair % 3]
                        acca, accg, accad, accgd = accs[pair % 2]
                        pa, pg = (0, 1) if pair % 2 == 0 else (2, 3)
                        hoff = (pair % 2) * 4
                        load_wu(pair + 2)
                        pair += 1
                        hreads = [hd[tq]] + ([hd[tq - 1]] if tq > 0 else []) + ([hd[tq + 1]] if tq < 3 else [])
                        hb = 4 if (pair % 2) == 1 else 7
                        hoff = 0
                        for s in range(2):
                            for k in range(8):
                                E("pe", lambda e: e.matmul(banks[hb][:, hoff + 2 * s:hoff + 2 * s + 2], lhsT=wu[:, k, s, :],
                                                           rhs=hT[:, k, c0:c0 + 514:513], start=(k == 0 and s == 0), stop=(k == 7),
                                                           skip_group_check=True),
                                  reads=[wud] + hreads, writes=[bdep[hb]], sig=False)
                        for s, pb in ((0, pa), (1, pg)):
                            for k in range(8):
                                E("pe", lambda e: e.matmul(banks[pb][:, :], lhsT=wu[:, k, s, :], rhs=hT[:, k, c0 + 1:c0 + 513],
                                                           start=(k == 0), stop=(k == 7)),
                                  reads=[wud, hd[tq]], writes=[bdep[pb]], sig=(k == 7))
                        for s, pb, acc, accd in ((0, pa, acca, accad), (1, pg, accg, accgd)):
                            m = mp + s * NCH
                            P = banks[pb]
                            H = banks[hb]
                            E("act", lambda e: e.activation(out=acc[:], in_=P[:, :], func=AF.Identity, bias=cwT[:, 3, m:m + 1], scale=cwT[:, 1, m:m + 1]),
                              reads=[bdep[pb], cwd], writes=[accd])
                            E("dve", lambda e: e.scalar_tensor_tensor(out=acc[:, 1:512], in0=P[:, 0:511], scalar=cwT[:, 0, m:m + 1], in1=acc[:, 1:512],
                                                                      op0=ALU.mult, op1=ALU.add), reads=[bdep[pb], cwd], writes=[accd])
                            E("dve", lambda e: e.scalar_tensor_tensor(out=acc[:, 0:511], in0=P[:, 1:512], scalar=cwT[:, 2, m:m + 1], in1=acc[:, 0:511],
                                                                      op0=ALU.mult, op1=ALU.add), reads=[bdep[pb], cwd], writes=[accd])
                            hl = hoff + 2 * s
                            E("dve", lambda e: e.scalar_tensor_tensor(out=acc[:, 0:1], in0=H[:, hl:hl + 1], scalar=cwx[:, 0, m:m + 1], in1=acc[:, 0:1],
                                                                      op0=ALU.mult, op1=ALU.add), reads=[bdep[hb], cwd], writes=[accd])
                            E("dve", lambda e: e.scalar_tensor_tensor(out=acc[:, 511:512], in0=H[:, hl + 1:hl + 2], scalar=cwx[:, 1, m:m + 1], in1=acc[:, 511:512],
                                                                      op0=ALU.mult, op1=ALU.add), reads=[bdep[hb], cwd], writes=[accd])
                            E("dve", lambda e: e.scalar_tensor_tensor(out=acc[:, 256:257], in0=P[:, 255:256], scalar=cwx[:, 2, m:m + 1], in1=acc[:, 256:257],
                                                                      op0=ALU.mult, op1=ALU.add), reads=[bdep[pb], cwd], writes=[accd])
                            E("dve", lambda e: e.scalar_tensor_tensor(out=acc[:, 255:256], in0=P[:, 256:257], scalar=cwx[:, 3, m:m + 1], in1=acc[:, 255:256],
                                                                      op0=ALU.mult, op1=ALU.add), reads=[bdep[pb], cwd], writes=[accd])
                        E("act", lambda e: e.activation(out=acca[:], in_=acca[:], func=AF.Silu), reads=[accad], writes=[accad])
                        E("pool", lambda e: e.tensor_tensor(out=actT[:, mp, :], in0=acca[:], in1=accg[:], op=ALU.mult),
                          reads=[accad, accgd], writes=[actd[mp]])
                    for u in range(4):
                        tt = tq * 4 + u
                        for dh, b in ((0, 5), (1, 6)):
                            for k in range(NCH):
                                E("pe", lambda e: e.matmul(banks[b][:, :], lhsT=actT[:, k, u * 128:(u + 1) * 128], rhs=wd[:, k, dh * 512:(dh + 1) * 512],
                                                           start=(k == 0), stop=(k == NCH - 1)),
                                  reads=[actd[k], wdd], writes=[bdep[b]], sig=(k == NCH - 1))
                        post_norm(tt, 5, 6)
                mk.barrier()

        def attn_phase(i, j):
            lam_init = 0.8 - 0.6 * math.exp(-0.3 * i)
            with ExitStack() as ps:
                OT = T("OT", [128, 8, NT], BF16, ps)
                OTd = [[Dep() for _ in range(4)] for _ in range(8)]
                with ExitStack() as ph:
                    qa = [T(f"qa{t}", [128, NT], BF16, ph) for t in range(2)]
                    ka = [T(f"ka{t}", [128, NT + 256], BF16, ph) for t in range(2)]
                    qad = [Dep() for _ in range(4)]
                    kad = [Dep() for _ in range(5)]
                    auxd = Dep()
                    Va = T("Va", [128, 18, 130], BF16, ph)
                    Vad = [Dep() for _ in range(5)]
                    wq = T("wqkvh", [128, 8, 3, 128], BF16, ph)
                    wqd = Dep()
                    s_wq = mk.new_dma_sem(f"d_wq{i}")
                    cosT = T("cosT_sb", [128, NT], F32, ph)
                    sinT = T("sinT_sb", [128, NT], F32, ph)
                    ropd = Dep()
                    ets = [(T(f"e_{b}", [128, 1024], BF16, ph), Dep()) for b in range(2)]
                    qraw = T("qraw", [128, 512], BF16, ph)
                    qrawd = Dep()
                    t1 = lng[:, 0:512]
                    t2 = lnb[:, 0:512]
                    t1d, t2d = Dep(), Dep()
                    stg = [(gbt[:, s * 512:(s + 1) * 512].rearrange("p (a b) -> p a b", a=4), Dep()) for s in range(2)]
                    stgring = Ring(stg)
                    ckt = T("ckt", [128, 2, 128], BF16, ph)
                    cktd = Dep()
                    s_ck = mk.new_dma_sem(f"d_ck{i}")
                    s_cv = mk.new_dma_sem(f"d_cv{i}")
                    lp = T("lamp_sb", [128, 4, 64], F32, ph)
                    lpd = Dep()
                    gvb = T("gvb", [128, 128], F32, ph)
                    gvd = Dep()
                    of = T("of", [128, 128], F32, ph)
                    sq = T("sqt", [128, 128], F32, ph)
                    ofd = Dep()
                    onesb = T("onesb", [128, 128], BF16, ph)
                    gcol = T("gcol", [128, 1], F32, ph)
                    sA, sB = lng[:, 512:1024], lnb[:, 512:1024]
                    sAd, sBd = Dep(), Dep()
                    E("pool", lambda e: e.memset(onesb[:], 1.0), writes=[gvd])
                    E("sp", lambda e: e.dma_start(out=cosT[:], in_=cos_d), writes=[ropd], dma_sem="auto")
                    E("sp", lambda e: e.dma_start(out=sinT[:], in_=sin_d), writes=[ropd], dma_sem="auto")
                    E("sp", lambda e: e.dma_start(out=qa[0][64:128, :], in_=qaux_d), writes=[auxd], dma_sem="auto")
                    E("sp", lambda e: e.dma_start(out=qa[1][0:64, :], in_=qaux_d), writes=[auxd], dma_sem="auto")
                    E("sp", lambda e: e.dma_start(out=ka[0][64:128, :], in_=kaux_d), writes=[auxd], dma_sem="auto")
                    E("sp", lambda e: e.dma_start(out=ka[1][0:64, :], in_=kaux_d), writes=[auxd], dma_sem="auto")
                    E("pool", lambda e: e.memset(Va[:, :, 128:130], 1.0), writes=Vad)
                    E("sp", lambda e: e.dma_start(out=gcol[:], in_=subln_g[j].rearrange("(p o) -> p o", o=1)), writes=[gvd], dma_sem="auto")
                    E("dve", lambda e: e.tensor_scalar(out=gcol[:], in0=gcol[:], scalar1=float(1.0 - lam_init), scalar2=None, op0=ALU.mult),
                      reads=[gvd], writes=[gvd])
                    E("sp", lambda e: e.dma_start(out=lp[:].rearrange("p a b -> p (a b)"),
                                                  in_=lamp[j].rearrange("a b -> (a b)").partition_broadcast(128)), writes=[lpd], dma_sem="auto")
                    E("dve", lambda e: e.tensor_tensor(out=lp[:, 0, :], in0=lp[:, 0, :], in1=lp[:, 1, :], op=ALU.mult), reads=[lpd], writes=[lpd])
                    E("dve", lambda e: e.tensor_tensor(out=lp[:, 2, :], in0=lp[:, 2, :], in1=lp[:, 3, :], op=ALU.mult), reads=[lpd], writes=[lpd])
                    E("dve", lambda e: e.reduce_sum(out=misc[:, 6:7], in_=lp[:, 0, :], axis=AX.X), reads=[lpd], writes=[miscd])
                    E("dve", lambda e: e.reduce_sum(out=misc[:, 7:8], in_=lp[:, 2, :], axis=AX.X), reads=[lpd], writes=[miscd])
                    E("act", lambda e: e.activation(out=misc[:, 6:8], in_=misc[:, 6:8], func=AF.Exp), reads=[miscd], writes=[miscd])
                    E("dve", lambda e: e.tensor_tensor(out=misc[:, 5:6], in0=misc[:, 7:8], in1=misc[:, 6:7], op=ALU.subtract), reads=[miscd], writes=[miscd])
                    E("dve", lambda e: e.tensor_scalar(out=misc[:, 5:6], in0=misc[:, 5:6], scalar1=float(-lam_init), scalar2=None, op0=ALU.add),
                      reads=[miscd], writes=[miscd])
                    stopped = [att_stop == 1]
                    wsrc = w_qkv[j].rearrange("(k p) (s n) -> p k s n", p=128, s=3)
                    prering = Ring([0, 1, 2, 3, 7])
                    nkv = nk_d.rearrange("(a p) j f -> p a j f", p=128)
                    nvv = nv_d.rearrange("(a p) j f -> p a j f", p=128)
                    for h in range(8):
                        if stopped[0]:
                            break
                        for s_ in range(3):
                            E("pool", lambda e: e.dma_start(out=wq[:, :, s_, :], in_=wsrc[:, :, s_, h * 128:(h + 1) * 128]), writes=[wqd], dma_sem=s_wq)
                        E("pool", lambda e: e.dma_start(out=ckt[:], in_=ck_d[j][:, h * 128:(h + 1) * 128].rearrange("(a p) n -> p a n", p=128)),
                          writes=[cktd], dma_sem=s_ck)
                        E("pool", lambda e: e.dma_start(out=Va[:, 16:18, 0:128], in_=cv_d[j][:, h * 128:(h + 1) * 128].rearrange("(a p) n -> p a n", p=128)),
                          writes=[Vad[4]], dma_sem=s_cv)
                        if att_stop == 21:
                            stopped[0] = True
                            break
                        for which, dst, dstd in ((0, qa, qad), (1, ka, kad)):
                            for tg in range(4):
                                b = prering.next()
                                for k in range(8):
                                    E("pe", lambda e: e.matmul(banks[b][:, :], lhsT=wq[:, k, which, :], rhs=hT[:, k, 1 + tg * 512:1 + (tg + 1) * 512],
                                                               start=(k == 0), stop=(k == 7)),
                                      reads=[wqd, hd[tg]], writes=[bdep[b]], sig=(k == 7))
                                E("act", lambda e: e.activation(out=qraw[:], in_=banks[b][:, :], func=AF.Copy), reads=[bdep[b]], writes=[qrawd])
                                b2 = prering.next()
                                E("pe", lambda e: e.matmul(banks[b2][:, :], lhsT=rotm[:], rhs=qraw[:], start=True, stop=True),
                                  reads=[cst, qrawd], writes=[bdep[b2]])
                                cs = slice(tg * 512, (tg + 1) * 512)
                                E("dve", lambda e: e.tensor_tensor(out=t1, in0=banks[b][:, :], in1=cosT[:, cs], op=ALU.mult),
                                  reads=[bdep[b], ropd], writes=[t1d])
                                E("dve", lambda e: e.tensor_tensor(out=t2, in0=banks[b2][:, :], in1=sinT[:, cs], op=ALU.mult),
                                  reads=[bdep[b2], ropd], writes=[t2d])
                                E("pool", lambda e: e.tensor_tensor(out=dst[0][0:64, cs], in0=t1[0:64, :], in1=t2[0:64, :], op=ALU.add),
                                  reads=[t1d, t2d], writes=[dstd[tg]])
                                E("pool", lambda e: e.tensor_tensor(out=dst[1][64:128, cs], in0=t1[64:128, :], in1=t2[64:128, :], op=ALU.add),
                                  reads=[t1d, t2d], writes=[dstd[tg]])
                        if att_stop == 22:
                            stopped[0] = True
                            break
                        b = prering.next()
                        pv = bbf(b)
                        for a in range(2):
                            E("pe", lambda e: e.transpose(out=pv[:, a * 128:(a + 1) * 128], in_=ckt[:, a, :], identity=ident[:]),
                              reads=[cktd, cst], writes=[bdep[b]], sig=(a == 1))
                        E("dve", lambda e: e.tensor_copy(out=ka[0][0:64, NT:NT + 256], in_=pv[0:64, 0:256]), reads=[bdep[b]], writes=[kad[4]])
                        E("dve", lambda e: e.tensor_copy(out=ka[1][64:128, NT:NT + 256], in_=pv[64:128, 0:256]), reads=[bdep[b]], writes=[kad[4]])
                        if att_stop == 23:
                            stopped[0] = True
                            break
                        for which, outv in ((2, nvv), (1, nkv)):
                            for g4 in range(4):
                                b = prering.next()
                                for u in range(4):
                                    tt = g4 * 4 + u
                                    for k in range(8):
                                        E("pe", lambda e: e.matmul(banks[b][:, u * 128:(u + 1) * 128], lhsT=hT[:, k, 1 + tt * 128:1 + (tt + 1) * 128],
                                                                   rhs=wq[:, k, which, :], start=(k == 0), stop=(k == 7)),
                                          reads=[wqd, hd[g4]], writes=[bdep[b]], sig=(k == 7 and u == 3))
                                src3 = banks[b][:, :].rearrange("p (a n) -> p a n", a=4)
                                if which == 2:
                                    E("act", lambda e: e.activation(out=Va[:, g4 * 4:(g4 + 1) * 4, 0:128], in_=src3, func=AF.Copy),
                                      reads=[bdep[b]], writes=[Vad[g4]])
                                if att_stop == 24:
                                    continue
                                sg, sgd = stgring.next()
                                E("dve", lambda e: e.tensor_copy(out=sg, in_=src3), reads=[bdep[b]], writes=[sgd])
                                if att_stop == 25:
                                    continue
                                E("sp", lambda e: e.dma_start(out=outv[:, g4 * 4:(g4 + 1) * 4, j, h * 128:(h + 1) * 128], in_=sg),
                                  reads=[sgd], dma_sem="auto")
                        if att_stop in (2, 24, 25):
                            stopped[0] = True
                            break
                        for qg in range(4):
                            qs_ = slice(qg * 512, (qg + 1) * 512)
                            def scores(kc):
                                kg = min(kc // 4, 4)
                                pb = (kc % 2) * 2
                                for t in range(2):
                                    E("pe", lambda e: e.matmul(banks[pb + t][:, :], lhsT=ka[t][:, kc * 128:(kc + 1) * 128], rhs=qa[t][:, qs_], start=True, stop=True),
                                      reads=[kad[kg], qad[qg], auxd], writes=[bdep[pb + t]])
                                et, etd = ets[kc % 2]
                                E("act", lambda e: e.activation(out=et[:], in_=psum[:, pb * 512:(pb + 2) * 512], func=AF.Exp, scale=0.125),
                                  reads=[bdep[pb], bdep[pb + 1]], writes=[etd])

                            def pv(kc):
                                kg = min(kc // 4, 4)
                                et, etd = ets[kc % 2]
                                for t in range(2):
                                    E("pe", lambda e: e.matmul(banks[4 + t][:, :], lhsT=Va[:, kc, 0:128], rhs=et[:, t * 512:(t + 1) * 512],
                                                               start=(kc == 0), stop=(kc == 17)),
                                      reads=[etd, Vad[kg]], writes=[bdep[4 + t]], sig=False)
                                    E("pe", lambda e: e.matmul(banks[6 + t][:, :], lhsT=onesb[:], rhs=et[:, t * 512:(t + 1) * 512],
                                                               start=(kc == 0), stop=(kc == 17)),
                                      reads=[etd, gvd], writes=[bdep[6 + t]], sig=(t == 1))

                            scores(0)
                            for kc in range(18):
                                if kc + 1 < 18:
                                    scores(kc + 1)
                                pv(kc)
                            for t, sc_, scd_ in ((0, sA, sAd), (1, sB, sBd)):
                                E("act", lambda e: e.activation(out=sc_, in_=banks[6 + t][:, :], func=AF.Ln), reads=[bdep[6 + t]], writes=[scd_])
                                E("act", lambda e: e.activation(out=sc_, in_=sc_, func=AF.Exp, scale=-1.0), reads=[scd_], writes=[scd_])
                            E("dve", lambda e: e.tensor_scalar(out=sB, in0=sB, scalar1=misc[:, 5:6], scalar2=None, op0=ALU.mult),
                              reads=[miscd], writes=[sBd])
                            E("dve", lambda e: e.tensor_tensor(out=sA, in0=banks[4][:, :], in1=sA, op=ALU.mult), reads=[bdep[4]], writes=[sAd])
                            E("dve", lambda e: e.tensor_tensor(out=sB, in0=banks[5][:, :], in1=sB, op=ALU.mult), reads=[bdep[5]], writes=[sBd])
                            E("dve", lambda e: e.tensor_tensor(out=sA, in0=sA, in1=sB, op=ALU.add), reads=[sBd], writes=[sAd])
                            E("dve", lambda e: e.tensor_tensor(out=sB, in0=sA, in1=sA, op=ALU.mult), reads=[sAd], writes=[sBd])
                            E("pe", lambda e: e.matmul(banks[6][:, :], lhsT=onesf[:], rhs=sB, start=True, stop=True),
                              reads=[cst, sBd], writes=[bdep[6]])
                            E("act", lambda e: e.activation(out=sB, in_=banks[6][:, :], func=AF.Ln, bias=misc[:, 4:5], scale=1.0 / 128.0),
                              reads=[bdep[6], miscd], writes=[sBd])
                            E("act", lambda e: e.activation(out=sB, in_=sB, func=AF.Exp, scale=-0.5), reads=[sBd], writes=[sBd])
                            E("dve", lambda e: e.scalar_tensor_tensor(out=OT[:, h, qs_], in0=sA, scalar=gcol[:, 0:1], in1=sB, op0=ALU.mult, op1=ALU.mult),
                              reads=[sAd, sBd, gvd], writes=[OTd[h][qg]])
                            if att_stop == 3:
                                stopped[0] = True
                                break
                    mk.barrier()
                if stopped[0]:
                    return
                wo = T("wo", [128, 8, D], BF16, ps)
                wod = Dep()
                s_wo = mk.new_dma_sem(f"d_wo{i}")
                E("pool", lambda e: e.dma_start(out=wo[:], in_=w_o[j].rearrange("(k p) n -> p k n", p=128)), writes=[wod], dma_sem=s_wo)
                make_gb(16, 0, 1)
                for k in range(8):
                    E("pool", lambda e: e.tensor_tensor(out=wo[:, k, :], in0=wo[:, k, :], in1=gbt[:], op=ALU.mult), reads=[gbd], writes=[wod])
                load_ln(0, i)
                for tt in range(16):
                    b0, b1 = pmring.next()
                    for dh, b in ((0, b0), (1, b1)):
                        for h in range(8):
                            E("pe", lambda e: e.matmul(banks[b][:, :], lhsT=OT[:, h, tt * 128:(tt + 1) * 128], rhs=wo[:, h, dh * 512:(dh + 1) * 512],
                                                       start=(h == 0), stop=(h == 7)),
                              reads=[OTd[h][tt // 4], wod], writes=[bdep[b]], sig=(h == 7))
                    post_norm(tt, b0, b1)
                mk.barrier()

        stage = 0
        for i in range(DEPTH if att_stop is None else 0):
            if stop is not None and stage >= stop:
                break
            ada_phase(i)
            for tg in range(4):
                ln_mod(tg, 8, 0)
            if i % 2 == 0:
                fourier_phase(i, i // 2)
            else:
                attn_phase(i, i // 2)
            stage += 1
            if stop is not None and stage >= stop:
                break
            for tg in range(4):
                ln_mod(tg, 32, 24)
            ffn_phase(i)
            stage += 1
        if att_stop is not None:
            try:
                ada_phase(1)
                for tg in range(4):
                    ln_mod(tg, 8, 0)
                attn_phase(1, 0)
            except StopBuild:
                pass
        yout = y_d.rearrange("(t p) d -> p t d", p=128)
        for tt in range(16):
            E("sp", lambda e: e.dma_start(out=yout[:, tt, :], in_=xt[:, tt, :]), reads=[xd[tt]], dma_sem=s_out)
        mk.finish()
        print("program: inst", mk.n_inst, "waits", mk.n_wait, {k: e.sem.total for k, e in mk.engs.items()},
              {i: s.total for i, s in enumerate(mk.dma_sems) if s.total > 2000})
    return nc


def _consts():
    bf = ml_dtypes.bfloat16
    c = {}
    c["ident"] = np.eye(128, dtype=np.float32).astype(bf)
    c["identf"] = np.eye(128, dtype=np.float32)
    R = np.zeros((128, 128), np.float32)
    for t in range(2):
        for a in range(2):
            for f in range(16):
                i0 = t * 64 + a * 32 + f
                i1 = i0 + 16
                R[i0, i1] = -1.0
                R[i1, i0] = 1.0
    c["rotm"] = np.ascontiguousarray(R.T).astype(bf)
    n = np.arange(256)
    ang = 2 * np.pi * np.outer(n, n) / 256.0
    Cc = np.cos(ang) / 16.0
    Sc = np.sin(ang) / 16.0
    ccs = np.zeros((128, 2, 512), np.float32)
    for kc in range(2):
        ccs[:, kc, 0:256] = Cc[kc * 128:(kc + 1) * 128, :]
        ccs[:, kc, 256:512] = Sc[kc * 128:(kc + 1) * 128, :]
    c["ccs"] = ccs.astype(bf)

    def dft_pack(Cn, Sn):
        out = np.zeros((16, 128, 16, 2, 128), np.float32)
        C4 = Cn.reshape(16, 128, 16, 128)
        S4 = Sn.reshape(16, 128, 16, 128)
        out[:, :, :, 0, :] = C4.transpose(2, 1, 0, 3)
        out[:, :, :, 1, :] = -S4.transpose(2, 1, 0, 3)
        return out.astype(bf)

    nn_ = np.arange(2048)
    ang = 2 * np.pi * ((np.outer(nn_, nn_)) % 2048) / 2048.0
    c["dft_s"] = dft_pack(np.cos(ang) / math.sqrt(2048.0), np.sin(ang) / math.sqrt(2048.0))
    Cb = np.zeros((2048, 2048))
    Sb = np.zeros((2048, 2048))
    a256 = 2 * np.pi * np.outer(n, n) / 256.0
    for s in range(8):
        Cb[s * 256:(s + 1) * 256, s * 256:(s + 1) * 256] = np.cos(a256) / 16.0
        Sb[s * 256:(s + 1) * 256, s * 256:(s + 1) * 256] = np.sin(a256) / 16.0
    c["dft_p"] = dft_pack(Cb, Sb)
    pos = np.arange(2048)
    row = (pos // 64).astype(np.float32)
    col = (pos % 64).astype(np.float32)
    inv = (1.0 / (np.float32(10000.0) ** (np.arange(16, dtype=np.float32) / np.float32(16)))).astype(np.float32)
    ar = row[:, None] * inv
    ac = col[:, None] * inv
    angr = np.concatenate([ar, ar, ac, ac], axis=-1).astype(np.float32)
    cos = np.cos(angr).astype(np.float32).T
    sin = np.sin(angr).astype(np.float32).T
    c["cos_s"] = np.ascontiguousarray(np.concatenate([cos, cos], 0))
    c["sin_s"] = np.ascontiguousarray(np.concatenate([sin, sin], 0))
    c["cos_p"] = np.ones((128, 2048), np.float32)
    c["sin_p"] = np.zeros((128, 2048), np.float32)
    qaux = np.zeros((64, 2048), np.float32)
    kaux = np.zeros((64, 2304), np.float32)
    for s in range(8):
        kaux[s, s * 256:(s + 1) * 256] = 1.0
        qaux[s, :] = NEG
        qaux[s, s * 256:(s + 1) * 256] = 0.0
    kaux[8, 2048:] = 1.0
    qaux[8, :] = NEG
    c["qaux_p"] = qaux.astype(bf)
    c["kaux_p"] = kaux.astype(bf)
    c["qaux_s"] = np.zeros((64, 2048), bf)
    c["kaux_s"] = np.zeros((64, 2304), bf)
    return c


_CACHE = {}


def kernel(x_prompt, x_sample, cache_k, cache_v, c, c_ctx, w_ada, b_ada, w_fourier, w_qkv,
           lambda_q1, lambda_k1, lambda_q2, lambda_k2, subln_g, w_o, w_up, conv_w, conv_b,
           w_down, ln1_g, ln1_b, ln2_g, ln2_b):
    f = np.float32
    A = lambda a: np.ascontiguousarray(np.asarray(a, dtype=f))
    if "nc" not in _CACHE:
        _CACHE["nc"] = build_program()
        _CACHE["c"] = _consts()
    nc = _CACHE["nc"]
    C = _CACHE["c"]
    x_prompt, x_sample = A(x_prompt), A(x_sample)
    cache_k, cache_v = A(cache_k), A(cache_v)
    shared = {
        "ccs": C["ccs"], "ident": C["ident"], "identf": C["identf"], "rotm": C["rotm"],
        "w_ada": A(w_ada), "b_ada": A(b_ada).reshape(4, 48, 128), "w_fourier": A(w_fourier), "w_qkv": A(w_qkv),
        "lamp": np.ascontiguousarray(np.stack([A(lambda_q1), A(lambda_k1), A(lambda_q2), A(lambda_k2)], axis=1)),
        "subln_g": A(subln_g), "w_o": A(w_o), "w_up": A(w_up),
        "conv_w": A(conv_w).reshape(4, 3, 44, 128), "conv_b": A(conv_b).reshape(4, 44, 128),
        "w_down": A(w_down), "ln1_g": A(ln1_g), "ln1_b": A(ln1_b), "ln2_g": A(ln2_g), "ln2_b": A(ln2_b),
    }
    in_maps = []
    for r in range(8):
        m = dict(shared)
        if r < 4:
            m["x"] = x_sample[r]
            m["cvec"] = A(c)[r].reshape(8, 128)
            m["ck"] = cache_k[r].reshape(2, 256, 1024)
            m["cv"] = cache_v[r].reshape(2, 256, 1024)
            m["flag"] = np.zeros((128, 1), f)
            m["qaux"], m["kaux"] = C["qaux_s"], C["kaux_s"]
            m["cosT"], m["sinT"] = C["cos_s"], C["sin_s"]
            m["dft"] = C["dft_s"]
        else:
            m["x"] = x_prompt[(r - 4) * 8:(r - 3) * 8].reshape(2048, 1024)
            m["cvec"] = A(c_ctx).reshape(8, 128)
            m["ck"] = np.zeros((2, 256, 1024), f)
            m["cv"] = np.zeros((2, 256, 1024), f)
            m["flag"] = np.ones((128, 1), f)
            m["qaux"], m["kaux"] = C["qaux_p"], C["kaux_p"]
            m["cosT"], m["sinT"] = C["cos_p"], C["sin_p"]
            m["dft"] = C["dft_p"]
        in_maps.append(m)
    res = run_bass_kernel_spmd(nc, in_maps, core_ids=list(range(8)))
    R = res.results
    y_sample = np.stack([np.asarray(R[r]["y"], dtype=f) for r in range(4)], 0)
    y_prompt = np.concatenate([np.asarray(R[r]["y"], dtype=f).reshape(8, 256, 1024) for r in range(4, 8)], 0)
    nk = np.concatenate([np.asarray(R[r]["nk"], dtype=f).reshape(8, 256, 2, 1024) for r in range(4, 8)], 0)
    nv = np.concatenate([np.asarray(R[r]["nv"], dtype=f).reshape(8, 256, 2, 1024) for r in range(4, 8)], 0)
    new_k = np.ascontiguousarray(nk.transpose(0, 2, 1, 3)).reshape(32, 2, 256, 8, 2, 64)
    new_v = np.ascontiguousarray(nv.transpose(0, 2, 1, 3)).reshape(32, 2, 256, 8, 128)
    return (y_prompt, y_sample, new_k, new_v)
```

```python
import math
from contextlib import ExitStack
import numpy as np
import ml_dtypes
import concourse.bass as bass
import concourse.mybir as mybir
from concourse.bass_utils import run_bass_kernel_spmd

F32 = mybir.dt.float32
BF16 = mybir.dt.bfloat16
AF = mybir.ActivationFunctionType
ALU = mybir.AluOpType
AX = mybir.AxisListType

D = 1024
NT = 2048
DEPTH = 4
DFF = 2816
NCH = 22
ALPHA = (2 * DEPTH) ** 0.25
LN_EPS = 1e-6
SUB_EPS = 1e-5
NEG = -30000.0


class Sem:
    def __init__(self, h, dma=False):
        self.h = h
        self.total = 0
        self.dma = dma


class Dep:
    __slots__ = ("w", "r", "x")

    def __init__(self, x=False):
        self.w = None
        self.r = []
        self.x = x


class Eng:
    def __init__(self, name, h, sem):
        self.name = name
        self.h = h
        self.sem = sem
        self.known = {}
        self.pending = False


class MK:
    def __init__(self, nc, stack):
        self.nc = nc
        self.stack = stack
        self.engs = {}
        for name, h in (("pe", nc.tensor), ("act", nc.scalar), ("dve", nc.vector),
                        ("pool", nc.gpsimd), ("sp", nc.sync)):
            s = Sem(stack.enter_context(nc.semaphore("s_" + name)))
            self.engs[name] = Eng(name, h, s)
        self.dma_sems = []
        self.depsem = {}
        self.n_inst = 0
        self.n_wait = 0

    def new_dma_sem(self, name):
        s = Sem(self.stack.enter_context(self.nc.semaphore(name)), dma=True)
        self.dma_sems.append(s)
        return s

    def _wait(self, eng, evs):
        need = {}
        for ev in evs:
            if ev is None:
                continue
            s, v = ev
            if s.dma:
                v = s.total
            if s is eng.sem and eng.name == "pe":
                continue
            if eng.known.get(s, 0) >= v:
                continue
            if need.get(s, 0) < v:
                need[s] = v
        for s, v in need.items():
            eng.h.wait_ge(s.h, v)
            eng.known[s] = v
            self.n_wait += 1

    def dsem(self, dep):
        k = id(dep)
        if k not in self.depsem:
            self.depsem[k] = (self.new_dma_sem(f"d_auto{len(self.depsem)}"), dep)
        return self.depsem[k][0]

    def emit(self, eng, fn, reads=(), writes=(), sig=True, dma_sem=None):
        e = self.engs[eng]
        if dma_sem == "auto":
            dma_sem = self.dsem(writes[0] if len(writes) else reads[0])
        if any(d.x for d in reads):
            writes = list(writes) + [d for d in reads if d.x]
            reads = [d for d in reads if not d.x]
        evs = []
        for d in reads:
            evs.append(d.w)
        for d in writes:
            if dma_sem is not None and d.w is not None and d.w[0] is dma_sem and not d.r:
                pass
            else:
                evs.append(d.w)
            evs.extend(d.r)
        self._wait(e, evs)
        inst = fn(e.h)
        self.n_inst += 1
        if dma_sem is not None:
            dma_sem.total += 16
            inst.then_inc(dma_sem.h, 16)
            ev = (dma_sem, dma_sem.total)
        elif sig:
            e.sem.total += 1
            inst.then_inc(e.sem.h, 1)
            ev = (e.sem, e.sem.total)
            e.pending = False
        else:
            ev = (e.sem, e.sem.total + 1)
            e.pending = True
        for d in reads:
            d.r.append(ev)
            if len(d.r) > 48:
                best = {}
                for s, v in d.r:
                    if best.get(s, 0) < v:
                        best[s] = v
                d.r = list(best.items())
        for d in writes:
            d.w = ev
            d.r = []
        return inst

    def barrier(self):
        sems = [e.sem for e in self.engs.values()] + self.dma_sems
        for e in self.engs.values():
            assert not e.pending, e.name
            for s in sems:
                if s is e.sem or s.total == 0:
                    continue
                if e.known.get(s, 0) >= s.total:
                    continue
                e.h.wait_ge(s.h, s.total)
                e.known[s] = s.total

    def finish(self):
        self.barrier()


class Ring:
    def __init__(self, items):
        self.items = list(items)
        self.i = 0

    def next(self):
        it = self.items[self.i % len(self.items)]
        self.i += 1
        return it


class StopBuild(Exception):
    pass


def build_program(stop=None, att_stop=None):
    nc = bass.Bass("TRN2", target_bir_lowering=False)

    def din(name, shape, dt=F32):
        return nc.dram_tensor(name, list(shape), dt, kind="ExternalInput").ap()

    def dout(name, shape):
        return nc.dram_tensor(name, list(shape), F32, kind="ExternalOutput").ap()

    x_d = din("x", [NT, D])
    c_d = din("cvec", [8, 128])
    ck_d = din("ck", [2, 256, D])
    cv_d = din("cv", [2, 256, D])
    flag_d = din("flag", [128, 1])
    qaux_d = din("qaux", [64, NT], BF16)
    kaux_d = din("kaux", [64, NT + 256], BF16)
    cos_d = din("cosT", [128, NT])
    sin_d = din("sinT", [128, NT])
    dft_d = din("dft", [16, 128, 16, 2, 128], BF16)
    ccs_d = din("ccs", [128, 2, 512], BF16)
    ident_d = din("ident", [128, 128], BF16)
    identf_d = din("identf", [128, 128])
    rotm_d = din("rotm", [128, 128], BF16)
    w_ada = din("w_ada", [DEPTH, D, 6 * D])
    b_ada = din("b_ada", [DEPTH, 48, 128])
    w_fourier = din("w_fourier", [2, D, D])
    w_qkv = din("w_qkv", [2, D, 3 * D])
    lamp = din("lamp", [2, 4, 64])
    subln_g = din("subln_g", [2, 128])
    w_o = din("w_o", [2, D, D])
    w_up = din("w_up", [DEPTH, D, 2 * DFF])
    conv_w = din("conv_w", [DEPTH, 3, 44, 128])
    conv_b = din("conv_b", [DEPTH, 44, 128])
    w_down = din("w_down", [DEPTH, DFF, D])
    ln_g = [din("ln1_g", [DEPTH, D]), din("ln2_g", [DEPTH, D])]
    ln_b = [din("ln1_b", [DEPTH, D]), din("ln2_b", [DEPTH, D])]
    y_d = dout("y", [NT, D])
    nk_d = dout("nk", [NT, 2, D])
    nv_d = dout("nv", [NT, 2, D])

    with ExitStack() as st:
        mk = MK(nc, st)
        E = mk.emit

        tcount = [0]

        def T(name, shape, dt, stack=st):
            tcount[0] += 1
            return stack.enter_context(nc.sbuf_tensor(f"{name}_{tcount[0]}", list(shape), dt))

        psum = st.enter_context(nc.psum_tensor("psum_all", [128, 4096], F32))
        banks = [psum[:, i * 512:(i + 1) * 512] for i in range(8)]
        bdep = [Dep(x=True) for _ in range(8)]

        def bbf(i):
            return banks[i].bitcast(BF16)

        xt = T("xt", [128, 16, D], F32)
        xd = [Dep() for _ in range(16)]
        hT = T("hT", [128, 8, NT + 2], BF16)
        hd = [Dep() for _ in range(4)]
        nt = T("nt", [128, 4, D], BF16)
        ntd = [Dep() for _ in range(4)]
        gbt = T("gbt", [128, D], F32)
        gbd = Dep()
        lng = T("lng", [128, D], F32)
        lnb = T("lnb", [128, D], F32)
        lnd = Dep()
        ident = T("ident_sb", [128, 128], BF16)
        identf = T("identf_sb", [128, 128], F32)
        onesf = T("onesf", [128, 128], F32)
        rotm = T("rotm_sb", [128, 128], BF16)
        cst = Dep()
        adaT = T("adaT", [128, 48], F32)
        adad = Dep()
        sct = T("sct", [128, 8], BF16)
        misc = T("misc", [128, 16], F32)
        miscd = Dep()
        NSV = 6
        svt = T("svt", [128, NSV, 16], F32)
        svring = Ring([(svt[:, s, :], Dep()) for s in range(NSV)])
        dgt = T("dgt", [128, 2, 128], F32)
        dgring = Ring([(dgt[:, s, :], Dep()) for s in range(2)])

        s_in = mk.new_dma_sem("d_in")
        s_out = mk.new_dma_sem("d_out")
        s_ln = mk.new_dma_sem("d_ln")
        s_misc = mk.new_dma_sem("d_misc")

        xin = x_d.rearrange("(t p) d -> p t d", p=128)
        for tt in range(16):
            E("sp", lambda e: e.dma_start(out=xt[:, tt, :], in_=xin[:, tt, :]), writes=[xd[tt]], dma_sem=s_in)
        E("sp", lambda e: e.dma_start(out=ident[:], in_=ident_d), writes=[cst], dma_sem="auto")
        E("sp", lambda e: e.dma_start(out=identf[:], in_=identf_d), writes=[cst], dma_sem="auto")
        E("sp", lambda e: e.dma_start(out=rotm[:], in_=rotm_d), writes=[cst], dma_sem="auto")
        E("sp", lambda e: e.dma_start(out=misc[:, 0:1], in_=flag_d), writes=[miscd], dma_sem="auto")
        E("pool", lambda e: e.memset(onesf[:], 1.0), writes=[cst])
        E("pool", lambda e: e.memset(hT[:, :, 0:1], 0.0), writes=hd)
        E("pool", lambda e: e.memset(hT[:, :, NT + 1:NT + 2], 0.0), writes=hd)
        E("dve", lambda e: e.tensor_scalar(out=misc[:, 1:2], in0=misc[:, 0:1], scalar1=-1.0, scalar2=1.0, op0=ALU.mult, op1=ALU.add),
          reads=[miscd], writes=[miscd])
        E("dve", lambda e: e.tensor_scalar(out=misc[:, 2:3], in0=misc[:, 0:1], scalar1=-1.0, scalar2=None, op0=ALU.mult),
          reads=[miscd], writes=[miscd])
        E("dve", lambda e: e.memset(misc[:, 3:4], LN_EPS), writes=[miscd])
        E("dve", lambda e: e.memset(misc[:, 4:5], SUB_EPS), writes=[miscd])
        with ExitStack() as ps:
            c8 = T("c8", [8, 128], F32, ps)
            c8d = Dep()
            E("sp", lambda e: e.dma_start(out=c8[:], in_=c_d), writes=[c8d], dma_sem="auto")
            E("pe", lambda e: e.transpose(out=banks[0][:, 0:8], in_=c8[:], identity=identf[0:8, 0:8]),
              reads=[c8d, cst], writes=[bdep[0]])
            E("act", lambda e: e.activation(out=sct[:], in_=banks[0][:, 0:8], func=AF.Silu), reads=[bdep[0]], writes=[adad])
            mk.barrier()

        def ln_stats(src, deps):
            sv, dep = svring.next()
            for c in range(2):
                E("dve", lambda e: e.bn_stats(out=sv[:, c * 6:(c + 1) * 6], in_=src[:, c * 512:(c + 1) * 512]),
                  reads=deps, writes=[dep])
            E("dve", lambda e: e.bn_aggr(out=sv[:, 12:14], in_=sv[:, 0:12]), reads=[dep], writes=[dep])
            E("dve", lambda e: e.tensor_scalar(out=sv[:, 14:15], in0=sv[:, 13:14], scalar1=LN_EPS, scalar2=None, op0=ALU.add),
              reads=[dep], writes=[dep])
            E("act", lambda e: e.activation(out=sv[:, 14:15], in_=sv[:, 14:15], func=AF.Sqrt), reads=[dep], writes=[dep])
            E("dve", lambda e: e.reciprocal(out=sv[:, 14:15], in_=sv[:, 14:15]), reads=[dep], writes=[dep])
            E("dve", lambda e: e.scalar_tensor_tensor(out=sv[:, 15:16], in0=sv[:, 12:13], scalar=-1.0, in1=sv[:, 14:15],
                                                      op0=ALU.mult, op1=ALU.mult), reads=[dep], writes=[dep])
            return sv[:, 14:15], sv[:, 15:16], dep

        def load_ln(which, i):
            E("sp", lambda e: e.dma_start(out=lng[:], in_=ln_g[which][i].partition_broadcast(128)), writes=[lnd], dma_sem=s_ln)
            E("sp", lambda e: e.dma_start(out=lnb[:], in_=ln_b[which][i].partition_broadcast(128)), writes=[lnd], dma_sem=s_ln)

        def post_norm(tt, b0, b1):
            for h, b in ((0, b0), (1, b1)):
                E("dve", lambda e: e.scalar_tensor_tensor(out=xt[:, tt, h * 512:(h + 1) * 512], in0=xt[:, tt, h * 512:(h + 1) * 512],
                                                          scalar=ALPHA, in1=banks[b][:, :], op0=ALU.mult, op1=ALU.add),
                  reads=[bdep[b]], writes=[xd[tt]])
            rstd, nb, dep = ln_stats(xt[:, tt, :], [xd[tt]])
            E("act", lambda e: e.activation(out=xt[:, tt, :], in_=xt[:, tt, :], func=AF.Identity, bias=nb, scale=rstd),
              reads=[dep], writes=[xd[tt]])
            E("pool", lambda e: e.tensor_tensor(out=xt[:, tt, :], in0=xt[:, tt, :], in1=lng[:], op=ALU.mult),
              reads=[lnd], writes=[xd[tt]])
            E("pool", lambda e: e.tensor_tensor(out=xt[:, tt, :], in0=xt[:, tt, :], in1=lnb[:], op=ALU.add),
              reads=[lnd], writes=[xd[tt]])

        tpring = Ring([7, 4])
        evring = Ring(["act", "dve"])

        def ln_mod(tg, sc0, sh0):
            for u in range(4):
                tt = tg * 4 + u
                rstd, nb, dep = ln_stats(xt[:, tt, :], [xd[tt]])
                E("act", lambda e: e.activation(out=nt[:, u, :], in_=xt[:, tt, :], func=AF.Identity, bias=nb, scale=rstd),
                  reads=[xd[tt], dep], writes=[ntd[u]])
            for j in range(8):
                b = tpring.next()
                pv = bbf(b)
                for u in range(4):
                    E("pe", lambda e: e.transpose(out=pv[:, u * 128:(u + 1) * 128], in_=nt[:, u, j * 128:(j + 1) * 128], identity=ident[:]),
                      reads=[ntd[u], cst], writes=[bdep[b]], sig=(u == 3))
                dst = hT[:, j, 1 + tg * 512:1 + (tg + 1) * 512]
                if evring.next() == "act":
                    E("act", lambda e: e.activation(out=dst, in_=pv[:, 0:512], func=AF.Identity,
                                                    bias=adaT[:, sh0 + j:sh0 + j + 1], scale=adaT[:, sc0 + j:sc0 + j + 1]),
                      reads=[bdep[b], adad], writes=[hd[tg]])
                else:
                    E("dve", lambda e: e.tensor_scalar(out=dst, in0=pv[:, 0:512], scalar1=adaT[:, sc0 + j:sc0 + j + 1],
                                                       scalar2=adaT[:, sh0 + j:sh0 + j + 1], op0=ALU.mult, op1=ALU.add),
                      reads=[bdep[b], adad], writes=[hd[tg]])

        def make_gb(g0, b0, b1):
            for j in range(8):
                dg, dgd = dgring.next()
                E("dve", lambda e: e.tensor_scalar(out=dg, in0=identf[:], scalar1=adaT[:, g0 + j:g0 + j + 1], scalar2=None, op0=ALU.mult),
                  reads=[cst, adad], writes=[dgd])
                b = b0 if j < 4 else b1
                E("pe", lambda e: e.matmul(banks[b][:, (j % 4) * 128:(j % 4 + 1) * 128], lhsT=onesf[:], rhs=dg, start=True, stop=True),
                  reads=[cst, dgd], writes=[bdep[b]])
            E("dve", lambda e: e.tensor_copy(out=gbt[:, 0:512], in_=banks[b0][:, :]), reads=[bdep[b0]], writes=[gbd])
            E("dve", lambda e: e.tensor_copy(out=gbt[:, 512:1024], in_=banks[b1][:, :]), reads=[bdep[b1]], writes=[gbd])

        def ada_phase(i):
            with ExitStack() as ps:
                slots = [(T(f"wada{s}", [128, 8, 512], BF16, ps), Dep(), mk.new_dma_sem(f"d_ada{i}_{s}")) for s in range(2)]
                bA = T("bA", [48, 128], F32, ps)
                bAd = Dep()
                bT = T("bT", [128, 48], F32, ps)
                E("sp", lambda e: e.dma_start(out=bA[:], in_=b_ada[i]), writes=[bAd], dma_sem="auto")
                first = True
                for blk in range(12):
                    sl, sd, ss = slots[blk % 2]
                    E("pool", lambda e: e.dma_start(out=sl[:], in_=w_ada[i][:, blk * 512:(blk + 1) * 512].rearrange("(k p) n -> p k n", p=128)),
                      writes=[sd], dma_sem=ss)
                    for cc in range(4):
                        j = blk * 4 + cc
                        for k in range(8):
                            E("pe", lambda e: e.matmul(banks[0][:, j:j + 1], lhsT=sl[:, k, cc * 128:(cc + 1) * 128], rhs=sct[:, k:k + 1],
                                                       start=first, stop=(k == 7), skip_group_check=True),
                              reads=[sd, adad], writes=[bdep[0]], sig=(k == 7 and cc == 3))
                            first = False
                E("pe", lambda e: e.transpose(out=banks[1][:, 0:48], in_=bA[:], identity=identf[0:48, 0:48]),
                  reads=[bAd, cst], writes=[bdep[1]])
                E("dve", lambda e: e.tensor_copy(out=bT[:], in_=banks[1][:, 0:48]), reads=[bdep[1]], writes=[bAd])
                E("dve", lambda e: e.tensor_tensor(out=adaT[:], in0=banks[0][:, 0:48], in1=bT[:], op=ALU.add),
                  reads=[bdep[0], bAd], writes=[adad])
                for c0 in (8, 32):
                    E("dve", lambda e: e.tensor_scalar(out=adaT[:, c0:c0 + 8], in0=adaT[:, c0:c0 + 8], scalar1=1.0, scalar2=None, op0=ALU.add),
                      reads=[adad], writes=[adad])
                mk.barrier()

        pmring = Ring([(5, 6), (2, 3)])
        cpring = Ring(["act", "dve"])

        def evac_copy(dst, src, reads, writes):
            if cpring.next() == "act":
                E("act", lambda e: e.activation(out=dst, in_=src, func=AF.Copy), reads=reads, writes=writes)
            else:
                E("dve", lambda e: e.tensor_copy(out=dst, in_=src), reads=reads, writes=writes)

        def fourier_phase(i, jf):
            with ExitStack() as ps:
                M = T("fM", [128, 8, D], BF16, ps)
                Md = Dep()
                Ap = T("fAp", [128, 16, D], BF16, ps)
                Apd = [Dep() for _ in range(16)]
                Bp = T("fBp", [128, 16, D], BF16, ps)
                Bpd = [Dep() for _ in range(16)]
                ccs = T("ccs_sb", [128, 2, 512], BF16, ps)
                ccd = Dep()
                wf = Bp
                s_wf = mk.new_dma_sem(f"d_wf{i}")
                s_dft = [mk.new_dma_sem(f"d_dft{i}_{s}") for s in range(4)]
                E("sp", lambda e: e.dma_start(out=ccs[:], in_=ccs_d), writes=[ccd], dma_sem="auto")
                E("pool", lambda e: e.dma_start(out=wf[:, 0:8, :], in_=w_fourier[jf].rearrange("(k p) n -> p k n", p=128)),
                  writes=Bpd[0:8], dma_sem=s_wf)
                make_gb(16, 0, 1)
                bring = Ring([0, 1, 2, 3])

                def make_M(coff):
                    for g in range(4):
                        for cch in range(2):
                            for dh in range(2):
                                b = bring.next()
                                for kc in range(2):
                                    E("pe", lambda e: e.matmul(banks[b][:, :], lhsT=ccs[:, kc, coff + cch * 128:coff + (cch + 1) * 128],
                                                               rhs=wf[:, g * 2 + kc, dh * 512:(dh + 1) * 512], start=(kc == 0), stop=(kc == 1)),
                                      reads=[ccd] + Bpd[0:8], writes=[bdep[b]], sig=(kc == 1))
                                E("dve", lambda e: e.tensor_tensor(out=M[:, g * 2 + cch, dh * 512:(dh + 1) * 512], in0=banks[b][:, :],
                                                                   in1=gbt[:, dh * 512:(dh + 1) * 512], op=ALU.mult),
                                  reads=[bdep[b], gbd], writes=[Md])

                def project(dstT, dstd):
                    for tt in range(16):
                        for dh in range(2):
                            b = bring.next()
                            for k in range(8):
                                E("pe", lambda e: e.matmul(banks[b][:, :], lhsT=hT[:, k, 1 + tt * 128:1 + (tt + 1) * 128],
                                                           rhs=M[:, k, dh * 512:(dh + 1) * 512], start=(k == 0), stop=(k == 7)),
                                  reads=[hd[tt // 4], Md], writes=[bdep[b]], sig=(k == 7))
                            evac_copy(dstT[:, tt, dh * 512:(dh + 1) * 512], banks[b][:, :], [bdep[b]], [dstd[tt]])

                make_M(0)
                project(Ap, Apd)
                make_M(256)
                project(Bp, Bpd)
                hflat = hT[:].rearrange("p a b -> p (a b)")
                slots = [hflat[:, s * 4096:(s + 1) * 4096].rearrange("p (k s n) -> p k s n", k=16, s=2) for s in range(4)]
                sld = [Dep() for _ in range(4)]
                load_ln(0, i)
                for n in range(16):
                    sl, sd = slots[n % 4], sld[n % 4]
                    E("sp", lambda e: e.dma_start(out=sl, in_=dft_d[n]), writes=[sd] + (hd if n < 4 else []), dma_sem=s_dft[n % 4])
                    b0, b1 = pmring.next()
                    for dh, b in ((0, b0), (1, b1)):
                        for k in range(16):
                            for s in range(2):
                                src, srcd = (Ap, Apd) if s == 0 else (Bp, Bpd)
                                E("pe", lambda e: e.matmul(banks[b][:, :], lhsT=sl[:, k, s, :], rhs=src[:, k, dh * 512:(dh + 1) * 512],
                                                           start=(k == 0 and s == 0), stop=(k == 15 and s == 1)),
                                  reads=[sd, srcd[k]], writes=[bdep[b]], sig=(k == 15 and s == 1))
                    post_norm(n, b0, b1)
                E("pool", lambda e: e.memset(hT[:, :, 0:1], 0.0), writes=sld + hd)
                E("pool", lambda e: e.memset(hT[:, :, NT + 1:NT + 2], 0.0), writes=sld + hd)
                mk.barrier()

        def ffn_phase(i):
            with ExitStack() as ps:
                wd = T("wd", [128, NCH, D], BF16, ps)
                wdd = Dep()
                actT = T("actT", [128, NCH, 512], BF16, ps)
                actd = [Dep() for _ in range(NCH)]
                wus = [(T(f"wu{s}", [128, 8, 2, 128], BF16, ps), Dep(), mk.new_dma_sem(f"d_wu{i}_{s}")) for s in range(3)]
                accs = [(T(f"acca{s}", [128, 512], F32, ps), T(f"accg{s}", [128, 512], F32, ps), Dep(), Dep()) for s in range(2)]
                cwr = gbt[0:44, 0:512].rearrange("p (a b) -> p a b", a=4)
                cwrd = gbd
                cwT = T("cwT", [128, 4, 44], F32, ps)
                cwx = T("cwx", [128, 4, 44], F32, ps)
                cwd = Dep()
                s_wd = mk.new_dma_sem(f"d_wd{i}")
                E("pool", lambda e: e.dma_start(out=wd[:], in_=w_down[i].rearrange("(k p) n -> p k n", p=128)), writes=[wdd], dma_sem=s_wd)
                for s in range(3):
                    E("sp", lambda e: e.dma_start(out=cwr[:, s, :], in_=conv_w[i, s]), writes=[cwrd], dma_sem="auto")
                E("sp", lambda e: e.dma_start(out=cwr[:, 3, :], in_=conv_b[i]), writes=[cwrd], dma_sem="auto")
                for s in range(4):
                    E("pe", lambda e: e.transpose(out=banks[4][:, s * 44:(s + 1) * 44], in_=cwr[:, s, :], identity=identf[0:44, 0:44]),
                      reads=[cwrd, cst], writes=[bdep[4]])
                E("dve", lambda e: e.tensor_copy(out=cwT[:].rearrange("p a b -> p (a b)"), in_=banks[4][:, 0:176]), reads=[bdep[4]], writes=[cwd])
                for (o, tap, mcol) in ((0, 0, 1), (1, 2, 1), (2, 0, 2), (3, 2, 2)):
                    E("dve", lambda e: e.tensor_scalar(out=cwx[:, o, :], in0=cwT[:, tap, :], scalar1=misc[:, mcol:mcol + 1], scalar2=None, op0=ALU.mult),
                      reads=[cwd, miscd], writes=[cwd])
                make_gb(40, 5, 6)
                for k in range(NCH):
                    E("pool", lambda e: e.tensor_tensor(out=wd[:, k, :], in0=wd[:, k, :], in1=gbt[:], op=ALU.mult),
                      reads=[gbd], writes=[wdd])
                load_ln(1, i)
                wup = w_up[i].rearrange("(k p) (s n) -> p k s n", p=128, s=2)

                def load_wu(pidx):
                    if pidx >= 4 * NCH:
                        return
                    wu_, wud_, sem_ = wus[pidx % 3]
                    mp_ = pidx % NCH
                    for s_ in range(2):
                        E("pool", lambda e: e.dma_start(out=wu_[:, :, s_, :], in_=wup[:, :, s_, mp_ * 128:(mp_ + 1) * 128]), writes=[wud_], dma_sem=sem_)

                load_wu(0)
                load_wu(1)
                pair = 0
                for tq in range(4):
                    c0 = tq * 512
                    for mp in range(NCH):
                        wu, wud, wus_sem = wus[pair % 3]
                        acca, accg, accad, accgd = accs[pair % 2]
                        pa, pg = (0, 1) if pair % 2 == 0 else (2, 3)
                        hoff = (pair % 2) * 4
                        load_wu(pair + 2)
                        pair += 1
                        hreads = [hd[tq]] + ([hd[tq - 1]] if tq > 0 else []) + ([hd[tq + 1]] if tq < 3 else [])
                        hb = 4 if (pair % 2) == 1 else 7
                        hoff = 0
                        for s in range(2):
                            for k in range(8):
                                E("pe", lambda e: e.matmul(banks[hb][:, hoff + 2 * s:hoff + 2 * s + 2], lhsT=wu[:, k, s, :],
                                                           rhs=hT[:, k, c0:c0 + 514:513], start=(k == 0 and s == 0), stop=(k == 7),
                                                           skip_group_check=True),
                                  reads=[wud] + hreads, writes=[bdep[hb]], sig=False)
                        for s, pb in ((0, pa), (1, pg)):
                            for k in range(8):
                                E("pe", lambda e: e.matmul(banks[pb][:, :], lhsT=wu[:, k, s, :], rhs=hT[:, k, c0 + 1:c0 + 513],
                                                           start=(k == 0), stop=(k == 7)),
                                  reads=[wud, hd[tq]], writes=[bdep[pb]], sig=(k == 7))
                        for s, pb, acc, accd in ((0, pa, acca, accad), (1, pg, accg, accgd)):
                            m = mp + s * NCH
                            P = banks[pb]
                            H = banks[hb]
                            E("act", lambda e: e.activation(out=acc[:], in_=P[:, :], func=AF.Identity, bias=cwT[:, 3, m:m + 1], scale=cwT[:, 1, m:m + 1]),
                              reads=[bdep[pb], cwd], writes=[accd])
                            E("dve", lambda e: e.scalar_tensor_tensor(out=acc[:, 1:512], in0=P[:, 0:511], scalar=cwT[:, 0, m:m + 1], in1=acc[:, 1:512],
                                                                      op0=ALU.mult, op1=ALU.add), reads=[bdep[pb], cwd], writes=[accd])
                            E("dve", lambda e: e.scalar_tensor_tensor(out=acc[:, 0:511], in0=P[:, 1:512], scalar=cwT[:, 2, m:m + 1], in1=acc[:, 0:511],
                                                                      op0=ALU.mult, op1=ALU.add), reads=[bdep[pb], cwd], writes=[accd])
                            hl = hoff + 2 * s
                            E("dve", lambda e: e.scalar_tensor_tensor(out=acc[:, 0:1], in0=H[:, hl:hl + 1], scalar=cwx[:, 0, m:m + 1], in1=acc[:, 0:1],
                                                                      op0=ALU.mult, op1=ALU.add), reads=[bdep[hb], cwd], writes=[accd])
                            E("dve", lambda e: e.scalar_tensor_tensor(out=acc[:, 511:512], in0=H[:, hl + 1:hl + 2], scalar=cwx[:, 1, m:m + 1], in1=acc[:, 511:512],
                                                                      op0=ALU.mult, op1=ALU.add), reads=[bdep[hb], cwd], writes=[accd])
                            E("dve", lambda e: e.scalar_tensor_tensor(out=acc[:, 256:257], in0=P[:, 255:256], scalar=cwx[:, 2, m:m + 1], in1=acc[:, 256:257],
                                                                      op0=ALU.mult, op1=ALU.add), reads=[bdep[pb], cwd], writes=[accd])
                            E("dve", lambda e: e.scalar_tensor_tensor(out=acc[:, 255:256], in0=P[:, 256:257], scalar=cwx[:, 3, m:m + 1], in1=acc[:, 255:256],
                                                                      op0=ALU.mult, op1=ALU.add), reads=[bdep[pb], cwd], writes=[accd])
                        E("act", lambda e: e.activation(out=acca[:], in_=acca[:], func=AF.Silu), reads=[accad], writes=[accad])
                        E("pool", lambda e: e.tensor_tensor(out=actT[:, mp, :], in0=acca[:], in1=accg[:], op=ALU.mult),
                          reads=[accad, accgd], writes=[actd[mp]])
                    for u in range(4):
                        tt = tq * 4 + u
                        for dh, b in ((0, 5), (1, 6)):
                            for k in range(NCH):
                                E("pe", lambda e: e.matmul(banks[b][:, :], lhsT=actT[:, k, u * 128:(u + 1) * 128], rhs=wd[:, k, dh * 512:(dh + 1) * 512],
                                                           start=(k == 0), stop=(k == NCH - 1)),
                                  reads=[actd[k], wdd], writes=[bdep[b]], sig=(k == NCH - 1))
                        post_norm(tt, 5, 6)
                mk.barrier()

        def attn_phase(i, j):
            lam_init = 0.8 - 0.6 * math.exp(-0.3 * i)
            with ExitStack() as ps:
                OT = T("OT", [128, 8, NT], BF16, ps)
                OTd = [[Dep() for _ in range(4)] for _ in range(8)]
                with ExitStack() as ph:
                    qa = [T(f"qa{t}", [128, NT], BF16, ph) for t in range(2)]
                    ka = [T(f"ka{t}", [128, NT + 256], BF16, ph) for t in range(2)]
                    qad = [Dep() for _ in range(4)]
                    kad = [Dep() for _ in range(5)]
                    auxd = Dep()
                    Va = T("Va", [128, 18, 130], BF16, ph)
                    Vad = [Dep() for _ in range(5)]
                    wq = T("wqkvh", [128, 8, 3, 128], BF16, ph)
                    wqd = Dep()
                    s_wq = mk.new_dma_sem(f"d_wq{i}")
                    cosT = T("cosT_sb", [128, NT], F32, ph)
                    sinT = T("sinT_sb", [128, NT], F32, ph)
                    ropd = Dep()
                    ets = [(T(f"e_{b}", [128, 1024], BF16, ph), Dep()) for b in range(2)]
                    qraw = T("qraw", [128, 512], BF16, ph)
                    qrawd = Dep()
                    t1 = lng[:, 0:512]
                    t2 = lnb[:, 0:512]
                    t1d, t2d = Dep(), Dep()
                    stg = [(gbt[:, s * 512:(s + 1) * 512].rearrange("p (a b) -> p a b", a=4), Dep()) for s in range(2)]
                    stgring = Ring(stg)
                    ckt = T("ckt", [128, 2, 128], BF16, ph)
                    cktd = Dep()
                    s_ck = mk.new_dma_sem(f"d_ck{i}")
                    s_cv = mk.new_dma_sem(f"d_cv{i}")
                    lp = T("lamp_sb", [128, 4, 64], F32, ph)
                    lpd = Dep()
                    gvb = T("gvb", [128, 128], F32, ph)
                    gvd = Dep()
                    of = T("of", [128, 128], F32, ph)
                    sq = T("sqt", [128, 128], F32, ph)
                    ofd = Dep()
                    onesb = T("onesb", [128, 128], BF16, ph)
                    gcol = T("gcol", [128, 1], F32, ph)
                    sA, sB = lng[:, 512:1024], lnb[:, 512:1024]
                    sAd, sBd = Dep(), Dep()
                    E("pool", lambda e: e.memset(onesb[:], 1.0), writes=[gvd])
                    E("sp", lambda e: e.dma_start(out=cosT[:], in_=cos_d), writes=[ropd], dma_sem="auto")
                    E("sp", lambda e: e.dma_start(out=sinT[:], in_=sin_d), writes=[ropd], dma_sem="auto")
                    E("sp", lambda e: e.dma_start(out=qa[0][64:128, :], in_=qaux_d), writes=[auxd], dma_sem="auto")
                    E("sp", lambda e: e.dma_start(out=qa[1][0:64, :], in_=qaux_d), writes=[auxd], dma_sem="auto")
                    E("sp", lambda e: e.dma_start(out=ka[0][64:128, :], in_=kaux_d), writes=[auxd], dma_sem="auto")
                    E("sp", lambda e: e.dma_start(out=ka[1][0:64, :], in_=kaux_d), writes=[auxd], dma_sem="auto")
                    E("pool", lambda e: e.memset(Va[:, :, 128:130], 1.0), writes=Vad)
                    E("sp", lambda e: e.dma_start(out=gcol[:], in_=subln_g[j].rearrange("(p o) -> p o", o=1)), writes=[gvd], dma_sem="auto")
                    E("dve", lambda e: e.tensor_scalar(out=gcol[:], in0=gcol[:], scalar1=float(1.0 - lam_init), scalar2=None, op0=ALU.mult),
                      reads=[gvd], writes=[gvd])
                    E("sp", lambda e: e.dma_start(out=lp[:].rearrange("p a b -> p (a b)"),
                                                  in_=lamp[j].rearrange("a b -> (a b)").partition_broadcast(128)), writes=[lpd], dma_sem="auto")
                    E("dve", lambda e: e.tensor_tensor(out=lp[:, 0, :], in0=lp[:, 0, :], in1=lp[:, 1, :], op=ALU.mult), reads=[lpd], writes=[lpd])
                    E("dve", lambda e: e.tensor_tensor(out=lp[:, 2, :], in0=lp[:, 2, :], in1=lp[:, 3, :], op=ALU.mult), reads=[lpd], writes=[lpd])
                    E("dve", lambda e: e.reduce_sum(out=misc[:, 6:7], in_=lp[:, 0, :], axis=AX.X), reads=[lpd], writes=[miscd])
                    E("dve", lambda e: e.reduce_sum(out=misc[:, 7:8], in_=lp[:, 2, :], axis=AX.X), reads=[lpd], writes=[miscd])
                    E("act", lambda e: e.activation(out=misc[:, 6:8], in_=misc[:, 6:8], func=AF.Exp), reads=[miscd], writes=[miscd])
                    E("dve", lambda e: e.tensor_tensor(out=misc[:, 5:6], in0=misc[:, 7:8], in1=misc[:, 6:7], op=ALU.subtract), reads=[miscd], writes=[miscd])
                    E("dve", lambda e: e.tensor_scalar(out=misc[:, 5:6], in0=misc[:, 5:6], scalar1=float(-lam_init), scalar2=None, op0=ALU.add),
                      reads=[miscd], writes=[miscd])
                    stopped = [att_stop == 1]
                    wsrc = w_qkv[j].rearrange("(k p) (s n) -> p k s n", p=128, s=3)
                    prering = Ring([0, 1, 2, 3, 7])
                    nkv = nk_d.rearrange("(a p) j f -> p a j f", p=128)
                    nvv = nv_d.rearrange("(a p) j f -> p a j f", p=128)
                    for h in range(8):
                        if stopped[0]:
                            break
                        for s_ in range(3):
                            E("pool", lambda e: e.dma_start(out=wq[:, :, s_, :], in_=wsrc[:, :, s_, h * 128:(h + 1) * 128]), writes=[wqd], dma_sem=s_wq)
                        E("pool", lambda e: e.dma_start(out=ckt[:], in_=ck_d[j][:, h * 128:(h + 1) * 128].rearrange("(a p) n -> p a n", p=128)),
                          writes=[cktd], dma_sem=s_ck)
                        E("pool", lambda e: e.dma_start(out=Va[:, 16:18, 0:128], in_=cv_d[j][:, h * 128:(h + 1) * 128].rearrange("(a p) n -> p a n", p=128)),
                          writes=[Vad[4]], dma_sem=s_cv)
                        if att_stop == 21:
                            stopped[0] = True
                            break
                        for which, dst, dstd in ((0, qa, qad), (1, ka, kad)):
                            for tg in range(4):
                                b = prering.next()
                                for k in range(8):
                                    E("pe", lambda e: e.matmul(banks[b][:, :], lhsT=wq[:, k, which, :], rhs=hT[:, k, 1 + tg * 512:1 + (tg + 1) * 512],
                                                               start=(k == 0), stop=(k == 7)),
                                      reads=[wqd, hd[tg]], writes=[bdep[b]], sig=(k == 7))
                                E("act", lambda e: e.activation(out=qraw[:], in_=banks[b][:, :], func=AF.Copy), reads=[bdep[b]], writes=[qrawd])
                                b2 = prering.next()
                                E("pe", lambda e: e.matmul(banks[b2][:, :], lhsT=rotm[:], rhs=qraw[:], start=True, stop=True),
                                  reads=[cst, qrawd], writes=[bdep[b2]])
                                cs = slice(tg * 512, (tg + 1) * 512)
                                E("dve", lambda e: e.tensor_tensor(out=t1, in0=banks[b][:, :], in1=cosT[:, cs], op=ALU.mult),
                                  reads=[bdep[b], ropd], writes=[t1d])
                                E("dve", lambda e: e.tensor_tensor(out=t2, in0=banks[b2][:, :], in1=sinT[:, cs], op=ALU.mult),
                                  reads=[bdep[b2], ropd], writes=[t2d])
                                E("pool", lambda e: e.tensor_tensor(out=dst[0][0:64, cs], in0=t1[0:64, :], in1=t2[0:64, :], op=ALU.add),
                                  reads=[t1d, t2d], writes=[dstd[tg]])
                                E("pool", lambda e: e.tensor_tensor(out=dst[1][64:128, cs], in0=t1[64:128, :], in1=t2[64:128, :], op=ALU.add),
                                  reads=[t1d, t2d], writes=[dstd[tg]])
                        if att_stop == 22:
                            stopped[0] = True
                            break
                        b = prering.next()
                        pv = bbf(b)
                        for a in range(2):
                            E("pe", lambda e: e.transpose(out=pv[:, a * 128:(a + 1) * 128], in_=ckt[:, a, :], identity=ident[:]),
                              reads=[cktd, cst], writes=[bdep[b]], sig=(a == 1))
                        E("dve", lambda e: e.tensor_copy(out=ka[0][0:64, NT:NT + 256], in_=pv[0:64, 0:256]), reads=[bdep[b]], writes=[kad[4]])
                        E("dve", lambda e: e.tensor_copy(out=ka[1][64:128, NT:NT + 256], in_=pv[64:128, 0:256]), reads=[bdep[b]], writes=[kad[4]])
                        if att_stop == 23:
                            stopped[0] = True
                            break
                        for which, outv in ((2, nvv), (1, nkv)):
                            for g4 in range(4):
                                b = prering.next()
                                for u in range(4):
                                    tt = g4 * 4 + u
                                    for k in range(8):
                                        E("pe", lambda e: e.matmul(banks[b][:, u * 128:(u + 1) * 128], lhsT=hT[:, k, 1 + tt * 128:1 + (tt + 1) * 128],
                                                                   rhs=wq[:, k, which, :], start=(k == 0), stop=(k == 7)),
                                          reads=[wqd, hd[g4]], writes=[bdep[b]], sig=(k == 7 and u == 3))
                                src3 = banks[b][:, :].rearrange("p (a n) -> p a n", a=4)
                                if which == 2:
                                    E("act", lambda e: e.activation(out=Va[:, g4 * 4:(g4 + 1) * 4, 0:128], in_=src3, func=AF.Copy),
                                      reads=[bdep[b]], writes=[Vad[g4]])
                                if att_stop == 24:
                                    continue
                                sg, sgd = stgring.next()
                                E("dve", lambda e: e.tensor_copy(out=sg, in_=src3), reads=[bdep[b]], writes=[sgd])
                                if att_stop == 25:
                                    continue
                                E("sp", lambda e: e.dma_start(out=outv[:, g4 * 4:(g4 + 1) * 4, j, h * 128:(h + 1) * 128], in_=sg),
                                  reads=[sgd], dma_sem="auto")
                        if att_stop in (2, 24, 25):
                            stopped[0] = True
                            break
                        for qg in range(4):
                            qs_ = slice(qg * 512, (qg + 1) * 512)
                            def scores(kc, qq=None):
                                qq = qg if qq is None else qq
                                qsl = slice(qq * 512, (qq + 1) * 512)
                                kg = min(kc // 4, 4)
                                pb = (kc % 2) * 2
                                for t in range(2):
                                    E("pe", lambda e: e.matmul(banks[pb + t][:, :], lhsT=ka[t][:, kc * 128:(kc + 1) * 128], rhs=qa[t][:, qsl], start=True, stop=True),
                                      reads=[kad[kg], qad[qq], auxd], writes=[bdep[pb + t]])
                                et, etd = ets[kc % 2]
                                E("act", lambda e: e.activation(out=et[:], in_=psum[:, pb * 512:(pb + 2) * 512], func=AF.Exp, scale=0.125),
                                  reads=[bdep[pb], bdep[pb + 1]], writes=[etd])

                            def pv(kc):
                                kg = min(kc // 4, 4)
                                et, etd = ets[kc % 2]
                                for t in range(2):
                                    E("pe", lambda e: e.matmul(banks[4 + t][:, :], lhsT=Va[:, kc, 0:128], rhs=et[:, t * 512:(t + 1) * 512],
                                                               start=(kc == 0), stop=(kc == 17)),
                                      reads=[etd, Vad[kg]], writes=[bdep[4 + t]], sig=False)
                                    E("pe", lambda e: e.matmul(banks[6 + t][:, :], lhsT=onesb[:], rhs=et[:, t * 512:(t + 1) * 512],
                                                               start=(kc == 0), stop=(kc == 17)),
                                      reads=[etd, gvd], writes=[bdep[6 + t]], sig=(t == 1))

                            if qg == 0:
                                scores(0)
                            for kc in range(18):
                                if kc + 1 < 18:
                                    scores(kc + 1)
                                elif qg + 1 < 4:
                                    scores(0, qg + 1)
                                pv(kc)
                            for t, sc_, scd_ in ((0, sA, sAd), (1, sB, sBd)):
                                E("act", lambda e: e.activation(out=sc_, in_=banks[6 + t][:, :], func=AF.Ln), reads=[bdep[6 + t]], writes=[scd_])
                                E("act", lambda e: e.activation(out=sc_, in_=sc_, func=AF.Exp, scale=-1.0), reads=[scd_], writes=[scd_])
                            E("dve", lambda e: e.tensor_scalar(out=sB, in0=sB, scalar1=misc[:, 5:6], scalar2=None, op0=ALU.mult),
                              reads=[miscd], writes=[sBd])
                            E("dve", lambda e: e.tensor_tensor(out=sA, in0=banks[4][:, :], in1=sA, op=ALU.mult), reads=[bdep[4]], writes=[sAd])
                            E("dve", lambda e: e.tensor_tensor(out=sB, in0=banks[5][:, :], in1=sB, op=ALU.mult), reads=[bdep[5]], writes=[sBd])
                            E("dve", lambda e: e.tensor_tensor(out=sA, in0=sA, in1=sB, op=ALU.add), reads=[sBd], writes=[sAd])
                            E("dve", lambda e: e.tensor_tensor(out=sB, in0=sA, in1=sA, op=ALU.mult), reads=[sAd], writes=[sBd])
                            E("pe", lambda e: e.matmul(banks[6][:, :], lhsT=onesf[:], rhs=sB, start=True, stop=True),
                              reads=[cst, sBd], writes=[bdep[6]])
                            E("act", lambda e: e.activation(out=sB, in_=banks[6][:, :], func=AF.Ln, bias=misc[:, 4:5], scale=1.0 / 128.0),
                              reads=[bdep[6], miscd], writes=[sBd])
                            E("act", lambda e: e.activation(out=sB, in_=sB, func=AF.Exp, scale=-0.5), reads=[sBd], writes=[sBd])
                            E("dve", lambda e: e.scalar_tensor_tensor(out=OT[:, h, qs_], in0=sA, scalar=gcol[:, 0:1], in1=sB, op0=ALU.mult, op1=ALU.mult),
                              reads=[sAd, sBd, gvd], writes=[OTd[h][qg]])
                            if att_stop == 3:
                                stopped[0] = True
                                break
                    mk.barrier()
                if stopped[0]:
                    return
                wo = T("wo", [128, 8, D], BF16, ps)
                wod = Dep()
                s_wo = mk.new_dma_sem(f"d_wo{i}")
                E("pool", lambda e: e.dma_start(out=wo[:], in_=w_o[j].rearrange("(k p) n -> p k n", p=128)), writes=[wod], dma_sem=s_wo)
                make_gb(16, 0, 1)
                for k in range(8):
                    E("pool", lambda e: e.tensor_tensor(out=wo[:, k, :], in0=wo[:, k, :], in1=gbt[:], op=ALU.mult), reads=[gbd], writes=[wod])
                load_ln(0, i)
                for tt in range(16):
                    b0, b1 = pmring.next()
                    for dh, b in ((0, b0), (1, b1)):
                        for h in range(8):
                            E("pe", lambda e: e.matmul(banks[b][:, :], lhsT=OT[:, h, tt * 128:(tt + 1) * 128], rhs=wo[:, h, dh * 512:(dh + 1) * 512],
                                                       start=(h == 0), stop=(h == 7)),
                              reads=[OTd[h][tt // 4], wod], writes=[bdep[b]], sig=(h == 7))
                    post_norm(tt, b0, b1)
                mk.barrier()

        stage = 0
        for i in range(DEPTH if att_stop is None else 0):
            if stop is not None and stage >= stop:
                break
            ada_phase(i)
            for tg in range(4):
                ln_mod(tg, 8, 0)
            if i % 2 == 0:
                fourier_phase(i, i // 2)
            else:
                attn_phase(i, i // 2)
            stage += 1
            if stop is not None and stage >= stop:
                break
            for tg in range(4):
                ln_mod(tg, 32, 24)
            ffn_phase(i)
            stage += 1
        if att_stop is not None:
            try:
                ada_phase(1)
                for tg in range(4):
                    ln_mod(tg, 8, 0)
                attn_phase(1, 0)
            except StopBuild:
                pass
        yout = y_d.rearrange("(t p) d -> p t d", p=128)
        for tt in range(16):
            E("sp", lambda e: e.dma_start(out=yout[:, tt, :], in_=xt[:, tt, :]), reads=[xd[tt]], dma_sem=s_out)
        mk.finish()
        print("program: inst", mk.n_inst, "waits", mk.n_wait, {k: e.sem.total for k, e in mk.engs.items()},
              {i: s.total for i, s in enumerate(mk.dma_sems) if s.total > 2000})
    return nc


def _consts():
    bf = ml_dtypes.bfloat16
    c = {}
    c["ident"] = np.eye(128, dtype=np.float32).astype(bf)
    c["identf"] = np.eye(128, dtype=np.float32)
    R = np.zeros((128, 128), np.float32)
    for t in range(2):
        for a in range(2):
            for f in range(16):
                i0 = t * 64 + a * 32 + f
                i1 = i0 + 16
                R[i0, i1] = -1.0
                R[i1, i0] = 1.0
    c["rotm"] = np.ascontiguousarray(R.T).astype(bf)
    n = np.arange(256)
    ang = 2 * np.pi * np.outer(n, n) / 256.0
    Cc = np.cos(ang) / 16.0
    Sc = np.sin(ang) / 16.0
    ccs = np.zeros((128, 2, 512), np.float32)
    for kc in range(2):
        ccs[:, kc, 0:256] = Cc[kc * 128:(kc + 1) * 128, :]
        ccs[:, kc, 256:512] = Sc[kc * 128:(kc + 1) * 128, :]
    c["ccs"] = ccs.astype(bf)

    def dft_pack(Cn, Sn):
        out = np.zeros((16, 128, 16, 2, 128), np.float32)
        C4 = Cn.reshape(16, 128, 16, 128)
        S4 = Sn.reshape(16, 128, 16, 128)
        out[:, :, :, 0, :] = C4.transpose(2, 1, 0, 3)
        out[:, :, :, 1, :] = -S4.transpose(2, 1, 0, 3)
        return out.astype(bf)

    nn_ = np.arange(2048)
    ang = 2 * np.pi * ((np.outer(nn_, nn_)) % 2048) / 2048.0
    c["dft_s"] = dft_pack(np.cos(ang) / math.sqrt(2048.0), np.sin(ang) / math.sqrt(2048.0))
    Cb = np.zeros((2048, 2048))
    Sb = np.zeros((2048, 2048))
    a256 = 2 * np.pi * np.outer(n, n) / 256.0
    for s in range(8):
        Cb[s * 256:(s + 1) * 256, s * 256:(s + 1) * 256] = np.cos(a256) / 16.0
        Sb[s * 256:(s + 1) * 256, s * 256:(s + 1) * 256] = np.sin(a256) / 16.0
    c["dft_p"] = dft_pack(Cb, Sb)
    pos = np.arange(2048)
    row = (pos // 64).astype(np.float32)
    col = (pos % 64).astype(np.float32)
    inv = (1.0 / (np.float32(10000.0) ** (np.arange(16, dtype=np.float32) / np.float32(16)))).astype(np.float32)
    ar = row[:, None] * inv
    ac = col[:, None] * inv
    angr = np.concatenate([ar, ar, ac, ac], axis=-1).astype(np.float32)
    cos = np.cos(angr).astype(np.float32).T
    sin = np.sin(angr).astype(np.float32).T
    c["cos_s"] = np.ascontiguousarray(np.concatenate([cos, cos], 0))
    c["sin_s"] = np.ascontiguousarray(np.concatenate([sin, sin], 0))
    c["cos_p"] = np.ones((128, 2048), np.float32)
    c["sin_p"] = np.zeros((128, 2048), np.float32)
    qaux = np.zeros((64, 2048), np.float32)
    kaux = np.zeros((64, 2304), np.float32)
    for s in range(8):
        kaux[s, s * 256:(s + 1) * 256] = 1.0
        qaux[s, :] = NEG
        qaux[s, s * 256:(s + 1) * 256] = 0.0
    kaux[8, 2048:] = 1.0
    qaux[8, :] = NEG
    c["qaux_p"] = qaux.astype(bf)
    c["kaux_p"] = kaux.astype(bf)
    c["qaux_s"] = np.zeros((64, 2048), bf)
    c["kaux_s"] = np.zeros((64, 2304), bf)
    return c


_CACHE = {}


def kernel(x_prompt, x_sample, cache_k, cache_v, c, c_ctx, w_ada, b_ada, w_fourier, w_qkv,
           lambda_q1, lambda_k1, lambda_q2, lambda_k2, subln_g, w_o, w_up, conv_w, conv_b,
           w_down, ln1_g, ln1_b, ln2_g, ln2_b):
    f = np.float32
    A = lambda a: np.ascontiguousarray(np.asarray(a, dtype=f))
    if "nc" not in _CACHE:
        _CACHE["nc"] = build_program()
        _CACHE["c"] = _consts()
    nc = _CACHE["nc"]
    C = _CACHE["c"]
    x_prompt, x_sample = A(x_prompt), A(x_sample)
    cache_k, cache_v = A(cache_k), A(cache_v)
    shared = {
        "ccs": C["ccs"], "ident": C["ident"], "identf": C["identf"], "rotm": C["rotm"],
        "w_ada": A(w_ada), "b_ada": A(b_ada).reshape(4, 48, 128), "w_fourier": A(w_fourier), "w_qkv": A(w_qkv),
        "lamp": np.ascontiguousarray(np.stack([A(lambda_q1), A(lambda_k1), A(lambda_q2), A(lambda_k2)], axis=1)),
        "subln_g": A(subln_g), "w_o": A(w_o), "w_up": A(w_up),
        "conv_w": A(conv_w).reshape(4, 3, 44, 128), "conv_b": A(conv_b).reshape(4, 44, 128),
        "w_down": A(w_down), "ln1_g": A(ln1_g), "ln1_b": A(ln1_b), "ln2_g": A(ln2_g), "ln2_b": A(ln2_b),
    }
    in_maps = []
    for r in range(8):
        m = dict(shared)
        if r < 4:
            m["x"] = x_sample[r]
            m["cvec"] = A(c)[r].reshape(8, 128)
            m["ck"] = cache_k[r].reshape(2, 256, 1024)
            m["cv"] = cache_v[r].reshape(2, 256, 1024)
            m["flag"] = np.zeros((128, 1), f)
            m["qaux"], m["kaux"] = C["qaux_s"], C["kaux_s"]
            m["cosT"], m["sinT"] = C["cos_s"], C["sin_s"]
            m["dft"] = C["dft_s"]
        else:
            m["x"] = x_prompt[(r - 4) * 8:(r - 3) * 8].reshape(2048, 1024)
            m["cvec"] = A(c_ctx).reshape(8, 128)
            m["ck"] = np.zeros((2, 256, 1024), f)
            m["cv"] = np.zeros((2, 256, 1024), f)
            m["flag"] = np.ones((128, 1), f)
            m["qaux"], m["kaux"] = C["qaux_p"], C["kaux_p"]
            m["cosT"], m["sinT"] = C["cos_p"], C["sin_p"]
            m["dft"] = C["dft_p"]
        in_maps.append(m)
    res = run_bass_kernel_spmd(nc, in_maps, core_ids=list(range(8)))
    R = res.results
    y_sample = np.stack([np.asarray(R[r]["y"], dtype=f) for r in range(4)], 0)
    y_prompt = np.concatenate([np.asarray(R[r]["y"], dtype=f).reshape(8, 256, 1024) for r in range(4, 8)], 0)
    nk = np.concatenate([np.asarray(R[r]["nk"], dtype=f).reshape(8, 256, 2, 1024) for r in range(4, 8)], 0)
    nv = np.concatenate([np.asarray(R[r]["nv"], dtype=f).reshape(8, 256, 2, 1024) for r in range(4, 8)], 0)
    new_k = np.ascontiguousarray(nk.transpose(0, 2, 1, 3)).reshape(32, 2, 256, 8, 2, 64)
    new_v = np.ascontiguousarray(nv.transpose(0, 2, 1, 3)).reshape(32, 2, 256, 8, 128)
    return (y_prompt, y_sample, new_k, new_v)
```
